# Optimizing a Trainium2 kernel written in Bass

```python
import math
import jax, jax.numpy as jnp
from jax import lax
import numpy as np

D_MODEL = 1024
BATCH = 8
SEQ = 4096
DEPTH = 1
DEC_BATCH = 32
DEC_SEQ = 1
PAST_LEN = 16384
PAGE_SIZE = 128

N_HEADS = 8
HEAD_DIM = 64
D_ATT = N_HEADS * HEAD_DIM
N_IDX_HEADS = 8
IDX_DIM = 64
TOPK_MAX = 256
D_RNN = D_MODEL - D_ATT
N_RNN_BLOCKS = 8
RNN_BLOCK = D_RNN // N_RNN_BLOCKS
CONV_W = 4
LRU_C = 8.0
N_BUCKETS = 32
MAX_DISTANCE = 128
D_FF = 4 * D_MODEL
DN_ALPHA = (2 * DEPTH) ** 0.25
DN_BETA = (8 * DEPTH) ** -0.25
LN_EPS = 1e-5
Q_BLOCK = 128
IN_SIZES = (D_ATT, D_ATT, D_ATT, N_IDX_HEADS * IDX_DIM, IDX_DIM, N_IDX_HEADS, D_RNN, D_RNN)
D_IN = sum(IN_SIZES)

kernel_name = "hymba_dsa_rglru_decoder_step"


def _split_in(proj):
    offs, acc = [], 0
    for s in IN_SIZES[:-1]:
        acc += s
        offs.append(acc)
    q, k, v, qi, ki, wi, xr, gr = jnp.split(proj, offs, axis=-1)
    lead = proj.shape[:-1]
    return (q.reshape(*lead, N_HEADS, HEAD_DIM), k.reshape(*lead, N_HEADS, HEAD_DIM),
            v.reshape(*lead, N_HEADS, HEAD_DIM), qi.reshape(*lead, N_IDX_HEADS, IDX_DIM),
            ki, wi, xr, gr)


def _layer_norm(x, g, b):
    xf = x.astype(jnp.float32)
    mu = jnp.mean(xf, axis=-1, keepdims=True)
    var = jnp.mean(jnp.square(xf - mu), axis=-1, keepdims=True)
    y = (xf - mu) * lax.rsqrt(var + LN_EPS) * g.astype(jnp.float32) + b.astype(jnp.float32)
    return y.astype(x.dtype)


def _t5_bucket(dist):
    dist = jnp.maximum(dist, 0)
    max_exact = N_BUCKETS // 2
    d = jnp.maximum(dist, 1).astype(jnp.float32)
    large = max_exact + (jnp.log(d / max_exact) / math.log(MAX_DISTANCE / max_exact)
                         * (N_BUCKETS - max_exact)).astype(jnp.int32)
    large = jnp.minimum(large, N_BUCKETS - 1)
    return jnp.where(dist < max_exact, dist, large)


def _take_rows(x, idx):
    return jax.vmap(lambda xb, ib: xb[ib])(x, idx)


def _indexer_scores(qi, wi, ki):
    dots = jnp.einsum('bqhd,bsd->bqhs', qi.astype(jnp.float32), ki.astype(jnp.float32))
    return jnp.einsum('bqh,bqhs->bqs', wi.astype(jnp.float32), jax.nn.relu(dots))


def _sparse_attend(q, k_sel, v_sel, bias, valid):
    logits = jnp.einsum('bqhd,bqkhd->bqhk', q.astype(jnp.float32), k_sel.astype(jnp.float32)) * (HEAD_DIM ** -0.5)
    logits = logits + jnp.moveaxis(bias.astype(jnp.float32), -1, 2)
    logits = jnp.where(valid[:, :, None, :], logits, -jnp.inf)
    p = jax.nn.softmax(logits, axis=-1)
    return jnp.einsum('bqhk,bqkhd->bqhd', p, v_sel.astype(jnp.float32)).astype(q.dtype)


def _prompt_attention(q, k, v, qi, ki, wi, rel_bias):
    B, S = q.shape[:2]
    topk = min(TOPK_MAX, S // 4)
    nb = S // Q_BLOCK

    def blocks(a):
        return jnp.moveaxis(a.reshape(B, nb, Q_BLOCK, *a.shape[2:]), 1, 0)

    key_pos = jnp.arange(S, dtype=jnp.int32)
    q_pos = key_pos.reshape(nb, Q_BLOCK)

    def one_block(args):
        q_b, qi_b, wi_b, pos_b = args
        scores = _indexer_scores(qi_b, wi_b, ki)
        causal = key_pos[None, :] <= pos_b[:, None]
        scores = jnp.where(causal[None], scores, -jnp.inf)
        _, idx = lax.top_k(scores, topk)
        dist = pos_b[None, :, None] - idx
        bias = rel_bias[_t5_bucket(dist)]
        return _sparse_attend(q_b, _take_rows(k, idx), _take_rows(v, idx), bias, dist >= 0)

    out = lax.map(one_block, (blocks(q), blocks(qi), blocks(wi), q_pos))
    return jnp.moveaxis(out, 0, 1).reshape(B, S, N_HEADS, HEAD_DIM)


def _sample_attention(q, k_new, v_new, qi, ki_new, wi, cache_k, cache_v, cache_ki, page_table, rel_bias):
    Bd, T = q.shape[:2]
    n_pages = PAST_LEN // PAGE_SIZE
    past = n_pages * PAGE_SIZE
    L = past + T
    topk = min(TOPK_MAX, L // 4)
    ki_past = cache_ki[page_table].reshape(Bd, past, IDX_DIM)
    ki_all = jnp.concatenate([ki_past, ki_new.astype(ki_past.dtype)], axis=1)
    q_pos = past + jnp.arange(T, dtype=jnp.int32)
    key_pos = jnp.arange(L, dtype=jnp.int32)
    scores = _indexer_scores(qi, wi, ki_all)
    scores = jnp.where(key_pos[None, None, :] <= q_pos[None, :, None], scores, -jnp.inf)
    _, idx = lax.top_k(scores, topk)
    from_past = (idx < past)[..., None, None]
    pidx = jnp.minimum(idx, past - 1)
    phys = jnp.take_along_axis(page_table, (pidx // PAGE_SIZE).reshape(Bd, -1), axis=1).reshape(idx.shape)
    off = pidx % PAGE_SIZE
    nidx = jnp.clip(idx - past, 0, T - 1)
    k_sel = jnp.where(from_past, cache_k[phys, off], _take_rows(k_new.astype(cache_k.dtype), nidx))
    v_sel = jnp.where(from_past, cache_v[phys, off], _take_rows(v_new.astype(cache_v.dtype), nidx))
    dist = q_pos[None, :, None] - idx
    bias = rel_bias[_t5_bucket(dist)]
    return _sparse_attend(q, k_sel, v_sel, bias, dist >= 0)


def _rglru_branch(xr, conv_prev, h_prev, conv_w, conv_b, w_a, b_a, w_x, b_x, lru_lambda):
    B, T = xr.shape[:2]
    x_ext = jnp.concatenate([conv_prev.astype(xr.dtype), xr], axis=1)
    xc = conv_b + x_ext[:, 0:T] * conv_w[0]
    for j in range(1, CONV_W):
        xc = xc + x_ext[:, j:j + T] * conv_w[j]
    new_conv = x_ext[:, T:]
    xb = xc.reshape(B, T, N_RNN_BLOCKS, RNN_BLOCK)
    r = jax.nn.sigmoid((jnp.einsum('btnc,ncd->btnd', xb, w_a) + b_a).astype(jnp.float32)).reshape(B, T, D_RNN)
    i = jax.nn.sigmoid((jnp.einsum('btnc,ncd->btnd', xb, w_x) + b_x).astype(jnp.float32)).reshape(B, T, D_RNN)
    log_a = -LRU_C * r * jax.nn.softplus(-lru_lambda.astype(jnp.float32))
    a = jnp.exp(log_a)
    u = jnp.sqrt(-jnp.expm1(2.0 * log_a)) * i * xc.astype(jnp.float32)
    u = u.at[:, 0].add(a[:, 0] * h_prev.astype(jnp.float32))

    def combine(left, right):
        a1, b1 = left
        a2, b2 = right
        return a1 * a2, a2 * b1 + b2

    _, h = lax.associative_scan(combine, (a, u), axis=1)
    return h, h[:, -1], new_conv


def _layer(x, attention_fn, conv_prev, h_prev, w_in, conv_w, conv_b, w_a, b_a, w_x, b_x, lru_lambda,
           w_out, ln1_g, ln1_b, w_up, b_up, w_down, b_down, ln2_g, ln2_b):
    B, T, _ = x.shape
    q, k, v, qi, ki, wi, xr, gr = _split_in(x @ w_in)
    attn = attention_fn(q, k, v, qi, ki, wi).reshape(B, T, D_ATT)
    h_seq, h_last, new_conv = _rglru_branch(xr, conv_prev, h_prev, conv_w, conv_b, w_a, b_a, w_x, b_x, lru_lambda)
    rnn = h_seq.astype(x.dtype) * jax.nn.gelu(gr)
    mix = jnp.concatenate([attn.astype(x.dtype), rnn], axis=-1) @ w_out
    x1 = _layer_norm(DN_ALPHA * x + mix, ln1_g, ln1_b)
    hid = jnp.square(jax.nn.relu(x1 @ w_up + b_up))
    x2 = _layer_norm(DN_ALPHA * x1 + hid @ w_down + b_down, ln2_g, ln2_b)
    return x2, (k, v, ki, h_last.astype(x.dtype), new_conv)


def setup_inputs(seed: int = 0) -> dict:
    key = jax.random.key(seed)
    ks = jax.random.split(key, 32)
    f32 = jnp.float32
    n_pages = PAST_LEN // PAGE_SIZE
    n_pool = (5 * DEC_BATCH * n_pages) // 4

    def nrm(k, shape, s=1.0):
        return s * jax.random.normal(k, shape, f32)

    x_prompt = nrm(ks[0], (BATCH, SEQ, D_MODEL))
    x_sample = nrm(ks[1], (DEC_BATCH, DEC_SEQ, D_MODEL))
    cache_k = nrm(ks[2], (DEPTH, n_pool, PAGE_SIZE, N_HEADS, HEAD_DIM))
    cache_v = nrm(ks[3], (DEPTH, n_pool, PAGE_SIZE, N_HEADS, HEAD_DIM))
    cache_k_idx = nrm(ks[4], (DEPTH, n_pool, PAGE_SIZE, IDX_DIM))
    state_h = nrm(ks[5], (DEPTH, DEC_BATCH, D_RNN), 0.5)
    state_conv = nrm(ks[6], (DEPTH, DEC_BATCH, CONV_W - 1, D_RNN))
    page_table = jax.random.permutation(ks[7], n_pool)[: DEC_BATCH * n_pages].reshape(DEC_BATCH, n_pages).astype(jnp.int32)
    rel_bias = nrm(ks[8], (N_BUCKETS, N_HEADS), 0.5)
    w_in = nrm(ks[9], (DEPTH, D_MODEL, D_IN), D_MODEL ** -0.5)
    conv_w = nrm(ks[10], (DEPTH, CONV_W, D_RNN), CONV_W ** -0.5)
    conv_b = nrm(ks[11], (DEPTH, D_RNN), 0.01)
    w_a = nrm(ks[12], (DEPTH, N_RNN_BLOCKS, RNN_BLOCK, RNN_BLOCK), RNN_BLOCK ** -0.5)
    b_a = nrm(ks[13], (DEPTH, N_RNN_BLOCKS, RNN_BLOCK), 0.01)
    w_x = nrm(ks[14], (DEPTH, N_RNN_BLOCKS, RNN_BLOCK, RNN_BLOCK), RNN_BLOCK ** -0.5)
    b_x = nrm(ks[15], (DEPTH, N_RNN_BLOCKS, RNN_BLOCK), 0.01)
    u = jax.random.uniform(ks[16], (DEPTH, D_RNN), f32, 0.9, 0.999)
    s = u ** (1.0 / LRU_C)
    lru_lambda = jnp.log(s) - jnp.log1p(-s)
    w_out = nrm(ks[17], (DEPTH, D_MODEL, D_MODEL), DN_BETA * D_MODEL ** -0.5)
    ln1_g = 1.0 + nrm(ks[18], (DEPTH, D_MODEL), 0.01)
    ln1_b = nrm(ks[19], (DEPTH, D_MODEL), 0.01)
    w_up = nrm(ks[20], (DEPTH, D_MODEL, D_FF), D_MODEL ** -0.5)
    b_up = nrm(ks[21], (DEPTH, D_FF), 0.01)
    w_down = nrm(ks[22], (DEPTH, D_FF, D_MODEL), DN_BETA * D_FF ** -0.5)
    b_down = nrm(ks[23], (DEPTH, D_MODEL), 0.01)
    ln2_g = 1.0 + nrm(ks[24], (DEPTH, D_MODEL), 0.01)
    ln2_b = nrm(ks[25], (DEPTH, D_MODEL), 0.01)
    return {"x_prompt": x_prompt, "x_sample": x_sample, "cache_k": cache_k, "cache_v": cache_v,
            "cache_k_idx": cache_k_idx, "state_h": state_h, "state_conv": state_conv,
            "page_table": page_table, "rel_bias": rel_bias, "w_in": w_in, "conv_w": conv_w,
            "conv_b": conv_b, "w_a": w_a, "b_a": b_a, "w_x": w_x, "b_x": b_x,
            "lru_lambda": lru_lambda, "w_out": w_out, "ln1_g": ln1_g, "ln1_b": ln1_b,
            "w_up": w_up, "b_up": b_up, "w_down": w_down, "b_down": b_down,
            "ln2_g": ln2_g, "ln2_b": ln2_b}


def reference(x_prompt, x_sample, cache_k, cache_v, cache_k_idx, state_h, state_conv, page_table,
              rel_bias, w_in, conv_w, conv_b, w_a, b_a, w_x, b_x, lru_lambda, w_out,
              ln1_g, ln1_b, w_up, b_up, w_down, b_down, ln2_g, ln2_b):
    y_p, y_s = x_prompt, x_sample
    kp, vp, kip, hp, cp = [], [], [], [], []
    ks_, vs_, kis, hs, cs = [], [], [], [], []
    zero_conv = jnp.zeros((x_prompt.shape[0], CONV_W - 1, D_RNN), x_prompt.dtype)
    zero_h = jnp.zeros((x_prompt.shape[0], D_RNN), x_prompt.dtype)
    for layer in range(DEPTH):
        lw = (w_in[layer], conv_w[layer], conv_b[layer], w_a[layer], b_a[layer], w_x[layer], b_x[layer],
              lru_lambda[layer], w_out[layer], ln1_g[layer], ln1_b[layer], w_up[layer], b_up[layer],
              w_down[layer], b_down[layer], ln2_g[layer], ln2_b[layer])
        prompt_attn = lambda q, k, v, qi, ki, wi: _prompt_attention(q, k, v, qi, ki, wi, rel_bias)
        ck, cv, cki = cache_k[layer], cache_v[layer], cache_k_idx[layer]
        sample_attn = lambda q, k, v, qi, ki, wi, ck=ck, cv=cv, cki=cki: _sample_attention(
            q, k, v, qi, ki, wi, ck, cv, cki, page_table, rel_bias)
        y_p, (k1, v1, ki1, h1, c1) = _layer(y_p, prompt_attn, zero_conv, zero_h, *lw)
        y_s, (k2, v2, ki2, h2, c2) = _layer(y_s, sample_attn, state_conv[layer], state_h[layer], *lw)
        kp.append(k1); vp.append(v1); kip.append(ki1); hp.append(h1); cp.append(c1)
        ks_.append(k2); vs_.append(v2); kis.append(ki2); hs.append(h2); cs.append(c2)
    k_prompt, v_prompt, k_idx_prompt = jnp.stack(kp), jnp.stack(vp), jnp.stack(kip)
    h_prompt, conv_prompt = jnp.stack(hp), jnp.stack(cp)
    k_sample, v_sample, k_idx_sample = jnp.stack(ks_), jnp.stack(vs_), jnp.stack(kis)
    h_sample, conv_sample = jnp.stack(hs), jnp.stack(cs)
    return (y_p, y_s, k_prompt, v_prompt, k_idx_prompt, h_prompt, conv_prompt,
            k_sample, v_sample, k_idx_sample, h_sample, conv_sample)
```

```python
import numpy as np
from contextlib import ExitStack
import concourse.bass as bass
import concourse.mybir as mybir
from concourse.bass_utils import run_bass_kernel_spmd

F32 = mybir.dt.float32
BF16 = mybir.dt.bfloat16
I32 = mybir.dt.int32
U8 = mybir.dt.uint8
AF = mybir.ActivationFunctionType
ALU = mybir.AluOpType
AX = mybir.AxisListType

D = 1024
DIN = 3144
DFF = 4096
ALPHA = 2.0 ** 0.25
EPS = 1e-5
NEG = -1.0e30
ENGS = ["pe", "act", "pool", "dve", "sp"]


class Sched:
    def __init__(self, nc, es, ndma=40):
        self.nc = nc
        self.lists = {e: [] for e in ENGS}
        self.seq = {e: 0 for e in ENGS}
        self.sem = {e: es.enter_context(nc.semaphore("sem_" + e)) for e in ENGS}
        self.ndma = ndma
        self.dsem = [es.enter_context(nc.semaphore("dsem%d" % i)) for i in range(ndma)]
        self.dval = [0] * ndma
        self.dlast = [None] * ndma
        self.dma_i = 0
        self.dma_ip = 0
        self.waited = {e: {} for e in ENGS}
        self.lastw = {}
        self.readers = {}

    def _semof(self, key):
        return self.sem[key[1]] if key[0] == "e" else self.dsem[key[1]]

    def _deps(self, eng, r, w, extra=()):
        needs = {}

        def need(tok, same_ok):
            if tok is None:
                return
            kind, ident, val = tok
            if kind == "e" and ident == eng and same_ok and eng == "pe":
                return
            k = (kind, ident)
            if needs.get(k, 0) < val:
                needs[k] = val

        for k in r:
            need(self.lastw.get(k), False)
        for k in w:
            need(self.lastw.get(k), True)
            for t in self.readers.get(k, {}).values():
                need(t, True)
        for t in extra:
            need(t, False)
        for k, val in needs.items():
            if self.waited[eng].get(k, 0) < val:
                self.waited[eng][k] = val
                sem = self._semof(k)
                self.lists[eng].append(lambda h, s=sem, v=val: h.wait_ge(s, v))

    def _mark(self, tok, r, w):
        for k in w:
            self.lastw[k] = tok
            self.readers[k] = {}
        for k in r:
            self.readers.setdefault(k, {})[(tok[0], tok[1])] = tok

    def op(self, eng, fn, r=(), w=()):
        self._deps(eng, r, w)
        self.seq[eng] += 1
        tok = ("e", eng, self.seq[eng])
        sem = self.sem[eng]
        self.lists[eng].append(lambda h, f=fn, s=sem: f(h).then_inc(s, 1))
        self._mark(tok, r, w)

    def dma(self, eng, fn, r=(), w=()):
        if eng == "pool":
            k = self.ndma - 8 + (self.dma_ip % 8)
            self.dma_ip += 1
        else:
            k = self.dma_i % (self.ndma - 8)
            self.dma_i += 1
        self._deps(eng, r, w, extra=(self.dlast[k],))
        self.dval[k] += 16
        tok = ("d", k, self.dval[k])
        self.dlast[k] = tok
        sem = self.dsem[k]
        self.lists[eng].append(lambda h, f=fn, s=sem: f(h).then_inc(s, 16))
        self._mark(tok, r, w)

    def flush(self, drain=True):
        nc = self.nc
        if drain:
            for k in range(self.ndma):
                t = self.dlast[k]
                if t is not None and self.waited["sp"].get(("d", k), 0) < t[2]:
                    self.waited["sp"][("d", k)] = t[2]
                    self.lists["sp"].append(lambda h, s=self.dsem[k], v=t[2]: h.wait_ge(s, v))
        lists = self.lists
        self.lists = {e: [] for e in ENGS}
        with nc.Block() as block:
            @block.tensor
            def _(h):
                for t in lists["pe"]:
                    t(h)

            @block.scalar
            def _(h):
                for t in lists["act"]:
                    t(h)

            @block.gpsimd
            def _(h):
                for t in lists["pool"]:
                    t(h)

            @block.vector
            def _(h):
                for t in lists["dve"]:
                    t(h)

            @block.sync
            def _(h):
                for t in lists["sp"]:
                    t(h)


def t5_bucket_np(dist):
    import math
    dist = np.maximum(dist, 0)
    d = np.maximum(dist, 1).astype(np.float32)
    large = 16 + (np.log(d / np.float32(16)) / np.float32(math.log(128 / 16)) * np.float32(16)).astype(np.int32)
    large = np.minimum(large, 31)
    return np.where(dist < 16, dist, large)


def make_consts(NPG=128):
    c = {}
    c["ident"] = np.eye(128, dtype=np.float32)
    q = np.arange(128)[:, None]
    s = np.arange(128)[None, :]
    c["tri"] = np.where(s <= q, 0.0, NEG).astype(np.float32)
    b = t5_bucket_np(np.arange(256))
    oh = np.zeros((32, 256), np.float32)
    oh[b, np.arange(256)] = 1.0
    oh[31, :] -= 1.0
    c["ohb"] = oh
    thr = np.array([int(np.argmax(b >= j)) for j in range(1, 32)], np.float32)
    c["bthr"] = np.tile(thr[None, :], (128, 1)).astype(np.float32)
    c["iota32"] = np.tile(np.arange(32, dtype=np.float32)[None, :], (128, 1))
    c["iota256"] = np.tile(np.arange(256, dtype=np.float32)[None, :], (128, 1))
    c["iotap"] = np.arange(128, dtype=np.float32)[:, None].copy()
    c["ones"] = np.ones((128, 128), np.float32)
    c["anti"] = np.eye(128, dtype=np.float32)[::-1].copy()
    su = np.triu(np.ones((128, 128), np.float32), 1)
    c["sut"] = su
    selfm = np.full((128, 1), NEG, np.float32)
    selfm[0, 0] = 0.0
    c["selfm"] = selfm
    p = np.arange(128, dtype=np.float32)[:, None]
    o = np.arange(129, dtype=np.float32)[None, :]
    pos = p * 128 + o
    pos[:, 128] = NPG * 128
    c["posc"] = pos.astype(np.float32)
    off = np.tile(o, (128, 1)).astype(np.float32)
    off[:, 128] = 0
    c["offc"] = off
    return c


CONST_SHAPES = {"ident": [128, 128], "tri": [128, 128], "ohb": [32, 256], "bthr": [128, 31],
                "anti": [128, 128], "iota32": [128, 32], "iota256": [128, 256], "iotap": [128, 1], "ones": [128, 128],
                "sut": [128, 128], "selfm": [128, 1], "posc": [128, 129], "offc": [128, 129]}


def build(S, NPG, NPOOL, with_sample=True, with_passB=True, NIT=19, stage=9):
    NT = S // 128
    TOPK = min(256, S // 4)
    nc = bass.Bass("TRN2", target_bir_lowering=False)

    def din(name, shape, dt=F32):
        return nc.dram_tensor(name, list(shape), dt, kind="ExternalInput").ap()

    def dout(name, shape, dt=F32):
        return nc.dram_tensor(name, list(shape), dt, kind="ExternalOutput").ap()

    x = din("x", [S, D]); xs = din("xs", [4, D])
    ck = din("ck", [NPOOL * 128, 512]); cv = din("cv", [NPOOL * 128, 512]); cki = din("cki", [NPOOL * 128, 64])
    sh = din("sh", [4, 512]); scv = din("scv", [4, 3, 512]); pt = din("pt", [4, NPG], I32)
    relb = din("relb", [32, 8]); w_in = din("w_in", [D, DIN]); conv_w = din("conv_w", [4, 512]); conv_b = din("conv_b", [512])
    w_a = din("w_a", [8, 64, 64]); b_a = din("b_a", [512]); w_x = din("w_x", [8, 64, 64]); b_x = din("b_x", [512])
    lam = din("lam", [512]); w_out = din("w_out", [D, D]); ln1_g = din("ln1_g", [D]); ln1_b = din("ln1_b", [D])
    w_up = din("w_up", [D, DFF]); b_up = din("b_up", [DFF]); w_down = din("w_down", [DFF, D]); b_down = din("b_down", [D])
    ln2_g = din("ln2_g", [D]); ln2_b = din("ln2_b", [D])
    C = {k: din("c_" + k, v) for k, v in CONST_SHAPES.items()}

    y = dout("y", [S, D]); ys = dout("ys", [4, D]); kp = dout("kp", [S, 512]); vp = dout("vp", [S, 512]); kip = dout("kip", [S, 64])
    hp = dout("hp", [512]); cp = dout("cp", [3, 512]); ksm = dout("ksm", [4, 512]); vsm = dout("vsm", [4, 512]); kis = dout("kis", [4, 64])
    hs = dout("hs", [4, 512]); cs = dout("cs", [4, 3, 512])
    catd = nc.dram_tensor("catd", [S + 128, D], BF16, kind="Internal").ap()
    tsc = nc.dram_tensor("tsc", [8, 512], F32, kind="Internal").ap()
    sst = nc.dram_tensor("sst", [4, 2120], F32, kind="Internal").ap()
    PAST = NPG * 128
    TOPK_S = min(256, (PAST + 1) // 4)

    with ExitStack() as es0:
        sc_ = Sched(nc, es0)
        op, dma = sc_.op, sc_.dma
        es0.enter_context(nc.allow_non_contiguous_dma(reason="small strided param loads"))

        def sb0(n, s, d=F32):
            return es0.enter_context(nc.sbuf_tensor(n, s, d))
        identb = sb0("identb", [128, 128], BF16)
        dma("pool", lambda h: h.dma_start(out=identb[:], in_=C["ident"]), w=["identb"])

        with ExitStack() as es:
            def sb(n, s, d=F32):
                return es.enter_context(nc.sbuf_tensor(n, s, d))

            def ps(n, s, d=F32):
                return es.enter_context(nc.psum_tensor(n, s, d))
            win = sb("win", [128, 8, DIN], BF16)
            KT = sb("KT", [128, 4, S], BF16)
            V = sb("V", [128, NT, 8, 65], BF16)
            KIT = sb("KIT", [128, S], BF16)
            MT = [sb("MT%d" % i, [128, NT, 128], BF16) for i in range(2)]
            scr = sb("scr", [128, S], F32)
            junk = sb("junk", [128, max(S, 4096)], U8)
            Rbuf = [junk[:, 1024 * b4:1024 * (b4 + 1)].bitcast(BF16) for b4 in range(4)]
            xin = sb("xin", [128, D], F32)
            xb = sb("xb", [128, D], BF16)
            Dg = xb
            xT = sb("xT", [128, 8, 128], BF16)
            QT = [sb("QT%d" % i, [128, 4, 128], BF16) for i in range(3)]
            QIT = sb("QIT", [128, 4, 128], BF16)
            kst = sb("kst", [128, 512]); vst = sb("vst", [128, 512]); kist = sb("kist", [128, 72])
            wiT = sb("wiT", [128, 8]); absw = sb("absw", [128, 8]); sgn = sb("sgn", [128, 8])
            E = [sb("E%d" % i, [128, 512], BF16) for i in range(2)]
            Pm = [sb("Pm%d" % i, [128, 512], BF16) for i in range(2)]
            mk = sb("mk", [128, 512], BF16)
            catA = sb("catA", [128, 512], BF16); catR = sb("catR", [128, 512], BF16)
            rnnT = sb("rnnT", [128, 4, 128], BF16)
            XR = sb("XR", [128, 4, 131]); GR = sb("GR", [128, 4, 128]); xc = sb("xc", [128, 4, 128])
            rr = sb("rr", [128, 4, 128]); ii = sb("ii", [128, 4, 128]); uu = sb("uu", [128, 4, 128])
            HH = sb("HH", [128, 4, 128]); xrc = sb("xrc", [128, 4, 3])
            EB = sb("EB", [128, 8, 2, 128], BF16)
            EBf = kst[:, 0:128]
            tri = sb("tri", [128, 128])
            WA = sb("WA", [128, 4, 128]); WX = sb("WX", [128, 4, 128])
            cw = sb("cw", [128, 4, 4]); cb = sb("cb", [128, 4]); ba = sb("ba", [128, 4]); bx = sb("bx", [128, 4])
            c8 = sb("c8", [128, 4]); hst = sb("hst", [128, 4])
            lo = sb("lo", [128, 1]); mid = sb("mid", [128, 1]); cnt = sb("cnt", [128, 1]); gg = sb("gg", [128, 1])
            rec = sb("rec", [128, 8])
            rb = scr[0:32, 0:8]; ohb = scr[0:32, 8:264]; Tt = vst[0:8, :]

            pA = ps("pA", [128, 512]); pB = ps("pB", [128, 512]); pT = ps("pT", [128, 1024], BF16)
            pI = [ps("pI0", [128, 512]), pB]
            pIk = ["pI0", "pB"]
            pS = [ps("pS%d" % i, [128, 512]) for i in range(2)]
            pOf = [ps("pO%d" % i, [128, 512]) for i in range(2)]
            pOt = [t[:, 0:260].rearrange("p (a d) -> p a d", a=4) for t in pOf]

            def pOv(hh):
                return pOt[hh // 4][:, hh % 4, :]

            def pOk(hh):
                return "pO%d" % (hh // 4)

            w_in_v = w_in.rearrange("(c p) n -> p c n", p=128)
            for c in range(8):
                for (a, b) in ((0, 1572), (1572, DIN)):
                    dma("pool", lambda h, c=c, a=a, b=b: h.dma_start(out=win[:, c, a:b], in_=w_in_v[:, c, a:b]), w=["win"])
            dma("sp", lambda h: h.dma_start(out=tri[:], in_=C["tri"]), w=["tri"])
            anti = sb("anti", [128, 128])
            dma("sp", lambda h: h.dma_start(out=anti[:], in_=C["anti"]), w=["anti"])
            op("pool", lambda h: h.memset(WA[:], 0.0), w=["WA"])
            op("pool", lambda h: h.memset(WX[:], 0.0), w=["WX"])
            for n in range(8):
                k, nl = n // 2, n % 2
                dma("sp", lambda h, n=n, k=k, nl=nl: h.dma_start(out=WA[nl * 64:(nl + 1) * 64, k, nl * 64:(nl + 1) * 64], in_=w_a[n]), w=["WA"])
                dma("sp", lambda h, n=n, k=k, nl=nl: h.dma_start(out=WX[nl * 64:(nl + 1) * 64, k, nl * 64:(nl + 1) * 64], in_=w_x[n]), w=["WX"])
            for j in range(4):
                dma("sp", lambda h, j=j: h.dma_start(out=cw[:, :, j], in_=conv_w[j].rearrange("(k p) -> p k", p=128)), w=["cw"])
            for nm, tl, src in (("cb", cb, conv_b), ("ba", ba, b_a), ("bx", bx, b_x), ("c8", c8, lam)):
                dma("sp", lambda h, tl=tl, src=src: h.dma_start(out=tl[:], in_=src.rearrange("(k p) -> p k", p=128)), w=[nm])
            op("act", lambda h: h.activation(out=c8[:], in_=c8[:], func=AF.Exp, scale=-1.0), r=["c8"], w=["c8"])
            op("act", lambda h: h.activation(out=c8[:], in_=c8[:], func=AF.Ln, bias=1.0), r=["c8"], w=["c8"])
            op("dve", lambda h: h.tensor_scalar(out=c8[:], in0=c8[:], scalar1=-8.0, scalar2=None, op0=ALU.mult), r=["c8"], w=["c8"])
            dma("sp", lambda h: h.dma_start(out=rb, in_=relb), w=[("scr", 0)])
            dma("sp", lambda h: h.dma_start(out=ohb, in_=C["ohb"]), w=[("scr", 0)])
            op("pe", lambda h: h.matmul(pA[0:8, 0:256], lhsT=rb, rhs=ohb, start=True, stop=True), r=[("scr", 0)], w=["pA"])
            op("dve", lambda h: h.memset(Tt, -30000.0), w=["vst"])
            op("dve", lambda h: h.tensor_copy(out=vst[0:8, 128:384], in_=pA[0:8, 0:256]), r=["pA"], w=["vst"])
            dma("sp", lambda h: h.dma_start(out=tsc, in_=Tt), r=["vst"], w=["tsc"])
            for hh in range(8):
                for dl in range(2):
                    src = bass.AP(tensor=tsc.tensor, offset=hh * 512 + 1 + 128 * dl, ap=[[1, 128], [1, 128]])
                    dma("sp", lambda h, src=src: h.dma_start(out=EBf, in_=src), r=["tsc"], w=["kst"])
                    op("pe", lambda h: h.matmul(pA[:, 0:128], lhsT=anti[:], rhs=EBf, start=True, stop=True), r=["anti", "kst"], w=["pA"])
                    op("act", lambda h, hh=hh, dl=dl: h.activation(out=EB[:, hh, dl, :], in_=pA[:, 0:128], func=AF.Exp), r=["pA"], w=["EB"])
            op("pool", lambda h: h.memset(V[:, :, :, 64:65], 1.0), w=["Vones"])
            op("pool", lambda h: h.memset(XR[:, :, 0:3], 0.0), w=["XR"])

            def load_x(i):
                dma("sp", lambda h, i=i: h.dma_start(out=xin[:], in_=x[i * 128:(i + 1) * 128, :]), w=["xin"])

            if stage >= 1:
                load_x(0)

            def fm_group(pbank, pkey, col0, nchunk, dup=False):
                if dup:
                    for hf in range(2):
                        for c in range(8):
                            op("pe", lambda h, hf=hf, c=c: h.matmul(pbank[hf * 64:(hf + 1) * 64, 0:128], lhsT=win[:, c, col0:col0 + 64], rhs=xT[:, c, :], start=(c == 0), stop=(c == 7)),
                               r=["win", "xT"], w=[pkey])
                    return
                for k in range(nchunk):
                    for c in range(8):
                        l = win[:, c, col0 + k * 128: col0 + (k + 1) * 128]
                        op("pe", lambda h, l=l, k=k, c=c: h.matmul(pbank[:, k * 128:(k + 1) * 128], lhsT=l, rhs=xT[:, c, :], start=(c == 0), stop=(c == 7)),
                           r=["win", "xT"], w=[pkey])

            def tm_group(pbank, pkey, col0, n):
                for c in range(8):
                    op("pe", lambda h, c=c: h.matmul(pbank[:, 0:n], lhsT=xT[:, c, :], rhs=win[:, c, col0:col0 + n], start=(c == 0), stop=(c == 7)),
                       r=["win", "xT"], w=[pkey])

            def stage_A(i):
                QTi = QT[i % 3]; qk = "QT%d" % (i % 3)
                op("act", lambda h: h.activation(out=xb[:], in_=xin[:], func=AF.Identity), r=["xin"], w=["xb"])
                if i + 1 < NT:
                    load_x(i + 1)
                for c in range(8):
                    op("pe", lambda h, c=c: h.transpose(pT[:, c * 128:(c + 1) * 128], xb[:, c * 128:(c + 1) * 128], identb[:]), r=["xb", "identb"], w=["pT"])
                op("act", lambda h: h.activation(out=xT[:].rearrange("p c t -> p (c t)"), in_=pT[:, :], func=AF.Identity), r=["pT"], w=["xT"])
                fm_group(pA, "pA", 0, 4)
                op("act", lambda h: h.activation(out=QTi[:].rearrange("p k t -> p (k t)"), in_=pA[:, :], func=AF.Identity, scale=0.125), r=["pA"], w=[qk])
                fm_group(pB, "pB", 512, 4)
                op("act", lambda h: h.activation(out=KT[:, :, i * 128:(i + 1) * 128], in_=pB[:, :].rearrange("p (k t) -> p k t", k=4), func=AF.Identity), r=["pB"], w=[("KT", i)])
                fm_group(pA, "pA", 1536, 4)
                op("act", lambda h: h.activation(out=QIT[:].rearrange("p k t -> p (k t)"), in_=pA[:, :], func=AF.Identity), r=["pA"], w=["QIT"])
                fm_group(pB, "pB", 2048, 1, dup=True)
                op("act", lambda h: h.activation(out=KIT[:, i * 128:(i + 1) * 128], in_=pB[:, 0:128], func=AF.Identity), r=["pB"], w=[("KIT", i)])
                fm_group(pA, "pA", 2120, 4)
                op("act", lambda h: h.activation(out=XR[:, :, 3:131], in_=pA[:, :].rearrange("p (k t) -> p k t", k=4), func=AF.Identity), r=["pA"], w=["XR"])
                fm_group(pB, "pB", 2632, 4)
                op("act", lambda h: h.activation(out=GR[:].rearrange("p k t -> p (k t)"), in_=pB[:, :], func=AF.Identity), r=["pB"], w=["GR"])
                tm_group(pA, "pA", 512, 512)
                op("act", lambda h: h.activation(out=kst[:], in_=pA[:, :], func=AF.Identity), r=["pA"], w=["kst"])
                dma("sp", lambda h: h.dma_start(out=kp[i * 128:(i + 1) * 128, :], in_=kst[:]), r=["kst"])
                tm_group(pB, "pB", 1024, 512)
                op("act", lambda h: h.activation(out=vst[:], in_=pB[:, :], func=AF.Identity), r=["pB"], w=["vst"])
                dma("sp", lambda h: h.dma_start(out=vp[i * 128:(i + 1) * 128, :], in_=vst[:]), r=["vst"])
                op("pool", lambda h: h.tensor_copy(out=V[:, i, :, 0:64], in_=vst[:].rearrange("p (a d) -> p a d", a=8)), r=["vst"], w=[("V", i)])
                tm_group(pA, "pA", 2048, 72)
                op("act", lambda h: h.activation(out=kist[:], in_=pA[:, 0:72], func=AF.Identity), r=["pA"], w=["kist"])
                dma("sp", lambda h: h.dma_start(out=kip[i * 128:(i + 1) * 128, :], in_=kist[:, 0:64]), r=["kist"])
                op("act", lambda h: h.activation(out=wiT[:], in_=kist[:, 64:72], func=AF.Identity), r=["kist"], w=["wiT"])

            def stage_I(i):
                L = 128 * (i + 1)
                nkc = (L + 511) // 512
                op("dve", lambda h: h.tensor_scalar(out=sgn[:], in0=wiT[:], scalar1=0.0, scalar2=2.0, op0=ALU.is_gt, op1=ALU.mult), r=["wiT"], w=["sgn"])
                op("dve", lambda h: h.tensor_scalar(out=sgn[:], in0=sgn[:], scalar1=-1.0, scalar2=None, op0=ALU.add), r=["sgn"], w=["sgn"])
                op("dve", lambda h: h.tensor_tensor(out=absw[:], in0=wiT[:], in1=sgn[:], op=ALU.mult), r=["wiT", "sgn"], w=["absw"])
                for hh in range(8):
                    op("dve", lambda h, hh=hh: h.tensor_scalar(out=Dg[:, hh * 128:(hh + 1) * 128], in0=identb[:], scalar1=sgn[:, hh:hh + 1], scalar2=None, op0=ALU.mult),
                       r=["identb", "sgn"], w=["xb"])
                units = [(kc, hh) for kc in range(nkc) for hh in range(8)]

                def emit_dots(ui):
                    kc, hh = units[ui]
                    wk = min(512, L - 512 * kc)
                    hq, pr = hh % 2, hh // 2
                    pb = pI[ui % 2]; pk = pIk[ui % 2]
                    ktl = [("KIT", j) for j in range(4 * kc, 4 * kc + wk // 128)]
                    op("pe", lambda h: h.matmul(pb[:, 0:wk], lhsT=QIT[hq * 64:(hq + 1) * 64, pr, :], rhs=KIT[hq * 64:(hq + 1) * 64, kc * 512:kc * 512 + wk], start=True, stop=True),
                       r=["QIT"] + ktl, w=[pk])

                emit_dots(0)
                for ui, (kc, hh) in enumerate(units):
                    wk = min(512, L - 512 * kc)
                    pb = pI[ui % 2]; pk = pIk[ui % 2]
                    Rb_ = Rbuf[ui % 4]; rk = ("junkR", ui % 4)
                    op("act", lambda h, pb=pb, Rb_=Rb_, wk=wk, hh=hh: h.activation(out=Rb_[:, 0:wk], in_=pb[:, 0:wk], func=AF.Relu, scale=absw[:, hh:hh + 1]), r=[pk, "absw", "junk"], w=[rk])
                    if ui + 1 < len(units):
                        emit_dots(ui + 1)
                    op("pe", lambda h, Rb_=Rb_, wk=wk, hh=hh: h.matmul(pA[:, 0:wk], lhsT=Dg[:, hh * 128:(hh + 1) * 128], rhs=Rb_[:, 0:wk], start=(hh == 0), stop=(hh == 7)),
                       r=["xb", rk], w=["pA"])
                    if hh == 7:
                        op("dve", lambda h, wk=wk, kc=kc: h.tensor_copy(out=scr[:, kc * 512:kc * 512 + wk], in_=pA[:, 0:wk]), r=["pA"], w=[("scr", kc)])
                kcd = i // 4
                op("dve", lambda h: h.tensor_tensor(out=scr[:, i * 128:(i + 1) * 128], in0=scr[:, i * 128:(i + 1) * 128], in1=tri[:], op=ALU.add), r=[("scr", kcd), "tri"], w=[("scr", kcd)])

            def stage_T(i):
                L = 128 * (i + 1)
                nkc = (L + 511) // 512
                sck = [("scr", kc) for kc in range(nkc)]
                jk = ["junk"] + [("junkR", b4) for b4 in range(4)]
                if L > TOPK:
                    op("dve", lambda h: h.memset(mid[:], 0.0), w=["mid"])
                    for it in range(NIT):
                        hk = 256.0 / (2.0 ** it)
                        op("dve", lambda h: h.tensor_scalar(out=junk[:, 0:L], in0=scr[:, 0:L], scalar1=mid[:, 0:1], scalar2=None, op0=ALU.is_gt, op1=ALU.add, accum_out=cnt[:, 0:1]),
                           r=sck + ["mid"], w=jk + ["cnt"])
                        op("dve", lambda h, hk=hk: h.tensor_scalar(out=gg[:], in0=cnt[:], scalar1=float(TOPK), scalar2=hk, op0=ALU.is_ge, op1=ALU.mult), r=["cnt"], w=["gg"])
                        op("dve", lambda h, hk=hk: h.scalar_tensor_tensor(out=mid[:], in0=mid[:], scalar=-hk / 2.0, in1=gg[:], op0=ALU.add, op1=ALU.add), r=["mid", "gg"], w=["mid"])
                    hN = 256.0 / (2.0 ** NIT)
                    op("dve", lambda h: h.tensor_scalar(out=lo[:], in0=mid[:], scalar1=-hN, scalar2=None, op0=ALU.add), r=["mid"], w=["lo"])
                else:
                    op("dve", lambda h: h.memset(lo[:], -1.0e29), w=["lo"])

            def stage_M(i):
                L = 128 * (i + 1)
                nkc = (L + 511) // 512
                MTi = MT[i % 2]
                for kc in range(nkc):
                    wk = min(512, L - 512 * kc)
                    nb = wk // 128
                    op("dve", lambda h, kc=kc, wk=wk: h.tensor_scalar(out=mk[:, 0:wk], in0=scr[:, kc * 512:kc * 512 + wk], scalar1=lo[:, 0:1], scalar2=-30000.0, op0=ALU.is_le, op1=ALU.mult),
                       r=[("scr", kc), "lo"], w=["mk"])
                    for jj in range(nb):
                        op("pe", lambda h, jj=jj: h.transpose(pT[:, jj * 128:(jj + 1) * 128], mk[:, jj * 128:(jj + 1) * 128], identb[:]), r=["mk", "identb"], w=["pT"])
                    op("act", lambda h, kc=kc, nb=nb, wk=wk: h.activation(out=MTi[:, 4 * kc:4 * kc + nb, :].rearrange("p j q -> p (j q)"), in_=pT[:, 0:wk], func=AF.Identity),
                       r=["pT"], w=[("MT", i % 2, kc)])

            def stage_AT(i):
                QTi = QT[i % 3]; qk = "QT%d" % (i % 3); MTi = MT[i % 2]
                units = [(hh, g) for hh in range(8) for g in range((i + 4) // 4)]

                def emit_qk(u, b2):
                    hh, g = u
                    hq, pr = hh % 2, hh // 2
                    j0 = 4 * g
                    n = min(j0 + 3, i) - j0 + 1
                    psb = pS[b2]
                    for jj in range(n):
                        j = j0 + jj
                        op("pe", lambda h, jj=jj, j=j: h.matmul(psb[:, jj * 128:(jj + 1) * 128], lhsT=KT[hq * 64:(hq + 1) * 64, pr, j * 128:(j + 1) * 128],
                                                              rhs=QTi[hq * 64:(hq + 1) * 64, pr, :], start=True, stop=False),
                           r=[("KT", j), qk], w=["pS%d" % b2])
                        op("pe", lambda h, jj=jj, j=j: h.matmul(psb[:, jj * 128:(jj + 1) * 128], lhsT=identb[:, :], rhs=MTi[:, j, :], start=False, stop=True),
                           r=["identb", ("MT", i % 2, g)], w=["pS%d" % b2])

                if units:
                    emit_qk(units[0], 0)
                for ui, (hh, g) in enumerate(units):
                    b2 = ui % 2
                    j0 = 4 * g
                    n = min(j0 + 3, i) - j0 + 1
                    psb = pS[b2]; Eb = E[b2]
                    op("act", lambda h, psb=psb, Eb=Eb, n=n: h.activation(out=Eb[:, 0:n * 128], in_=psb[:, 0:n * 128], func=AF.Exp), r=["pS%d" % b2], w=["E%d" % b2])
                    for jj in range(n):
                        j = j0 + jj
                        if i - j <= 1:
                            dl = i - j
                            op("pool", lambda h, Eb=Eb, jj=jj, hh=hh, dl=dl: h.tensor_tensor(out=Eb[:, jj * 128:(jj + 1) * 128], in0=Eb[:, jj * 128:(jj + 1) * 128],
                                                                                         in1=EB[:, hh, dl, :], op=ALU.mult),
                               r=["E%d" % b2, "EB"], w=["E%d" % b2])
                    if ui + 1 < len(units):
                        emit_qk(units[ui + 1], (ui + 1) % 2)
                    for jj in range(n):
                        j = j0 + jj
                        op("pe", lambda h, Eb=Eb, jj=jj, j=j, hh=hh, st=(j == 0), sp=(j == i): h.matmul(pOv(hh), lhsT=Eb[:, jj * 128:(jj + 1) * 128], rhs=V[:, j, hh, :], start=st, stop=sp),
                           r=["E%d" % b2, ("V", j), "Vones"], w=[pOk(hh)])

            def stage_ATn(i):
                for half in range(2):
                    pv = pOt[half]
                    op("dve", lambda h, pv=pv, half=half: h.reciprocal(out=rec[:, half * 4:(half + 1) * 4], in_=pv[:, :, 64]), r=["pO%d" % half], w=["rec"])
                    for a4 in range(4):
                        hh = half * 4 + a4
                        op("act", lambda h, pv=pv, a4=a4, hh=hh: h.activation(out=catA[:, hh * 64:(hh + 1) * 64], in_=pv[:, a4, 0:64], func=AF.Identity, scale=rec[:, hh:hh + 1]),
                           r=["pO%d" % half, "rec"], w=["catA"])
                dma("sp", lambda h: h.dma_start(out=catd[i * 128:(i + 1) * 128, 0:512], in_=catA[:]), r=["catA"], w=[("catd", i, 0)])

            UUK = [("uu", k) for k in range(4)]; XCK = [("xc", k) for k in range(4)]

            def stage_R1(i):
                for k in range(4):
                    op("dve", lambda h, k=k: h.tensor_scalar(out=xc[:, k, :], in0=XR[:, k, 3:131], scalar1=cw[:, k, 3:4], scalar2=cb[:, k:k + 1], op0=ALU.mult, op1=ALU.add),
                       r=["XR", "cw", "cb"], w=[("xc", k)])
                    for j in range(3):
                        op("dve", lambda h, k=k, j=j: h.scalar_tensor_tensor(out=xc[:, k, :], in0=XR[:, k, j:j + 128], scalar=cw[:, k, j:j + 1], in1=xc[:, k, :], op0=ALU.mult, op1=ALU.add),
                           r=["XR", "cw", ("xc", k)], w=[("xc", k)])
                op("pool", lambda h: h.tensor_copy(out=xrc[:], in_=XR[:, :, 128:131]), r=["XR"], w=["xrc"])
                if i == NT - 1:
                    for j in range(3):
                        dma("sp", lambda h, j=j: h.dma_start(out=cp[j].rearrange("(k p) -> p k", p=128), in_=xrc[:, :, j]), r=["xrc"])
                op("pool", lambda h: h.tensor_copy(out=XR[:, :, 0:3], in_=xrc[:]), r=["xrc"], w=["XR"])
                for k in range(4):
                    op("pe", lambda h, k=k: h.matmul(pA[:, k * 128:(k + 1) * 128], lhsT=WA[:, k, :], rhs=xc[:, k, :], start=True, stop=True), r=["WA", ("xc", k)], w=["pA"])
                    op("pe", lambda h, k=k: h.matmul(pB[:, k * 128:(k + 1) * 128], lhsT=WX[:, k, :], rhs=xc[:, k, :], start=True, stop=True), r=["WX", ("xc", k)], w=["pB"])
                for k in range(4):
                    op("act", lambda h, k=k: h.activation(out=rr[:, k, :], in_=pA[:, k * 128:(k + 1) * 128], func=AF.Sigmoid, bias=ba[:, k:k + 1]), r=["pA", "ba"], w=["rr"])
                    op("act", lambda h, k=k: h.activation(out=ii[:, k, :], in_=pB[:, k * 128:(k + 1) * 128], func=AF.Sigmoid, bias=bx[:, k:k + 1]), r=["pB", "bx"], w=["ii"])
                for k in range(4):
                    op("act", lambda h, k=k: h.activation(out=rr[:, k, :], in_=rr[:, k, :], func=AF.Exp, scale=c8[:, k:k + 1]), r=["rr", "c8"], w=["rr"])
                op("pool", lambda h: h.tensor_tensor(out=uu[:], in0=rr[:], in1=rr[:], op=ALU.mult), r=["rr"], w=UUK)
                op("act", lambda h: h.activation(out=uu[:], in_=uu[:], func=AF.Sqrt, scale=-1.0, bias=1.0), r=UUK, w=UUK)
                op("pool", lambda h: h.tensor_tensor(out=uu[:], in0=uu[:], in1=ii[:], op=ALU.mult), r=UUK + ["ii"], w=UUK)
                op("pool", lambda h: h.tensor_tensor(out=uu[:], in0=uu[:], in1=xc[:], op=ALU.mult), r=UUK + XCK, w=UUK)
                op("pool", lambda h: h.tensor_tensor(out=ii[:], in0=GR[:], in1=GR[:], op=ALU.mult), r=["GR"] + UUK, w=["ii"])
                op("pool", lambda h: h.tensor_scalar(out=ii[:], in0=ii[:], scalar1=0.044715, scalar2=1.0, op0=ALU.mult, op1=ALU.add), r=["ii"], w=["ii"])
                op("pool", lambda h: h.tensor_tensor(out=ii[:], in0=ii[:], in1=GR[:], op=ALU.mult), r=["ii", "GR"], w=["ii"])
                op("act", lambda h: h.activation(out=ii[:], in_=ii[:], func=AF.Sigmoid, scale=1.5957691216057308), r=["ii"], w=["ii"])
                op("pool", lambda h: h.tensor_tensor(out=ii[:], in0=ii[:], in1=GR[:], op=ALU.mult), r=["ii", "GR"], w=["ii"])

            def stage_R2(i):
                for k in range(4):
                    init = 0.0 if i == 0 else hst[:, k:k + 1]
                    op("dve", lambda h, k=k, init=init: h.tensor_tensor_scan(out=HH[:, k, :], data0=rr[:, k, :], data1=uu[:, k, :], initial=init, op0=ALU.mult, op1=ALU.add),
                       r=["rr", ("uu", k), "hst"], w=["HH"])
                op("dve", lambda h: h.tensor_copy(out=hst[:], in_=HH[:, :, 127]), r=["HH"], w=["hst"])
                if i == NT - 1:
                    dma("sp", lambda h: h.dma_start(out=hp.rearrange("(k p) -> p k", p=128), in_=hst[:]), r=["hst"])
                op("pool", lambda h: h.tensor_tensor(out=rnnT[:], in0=ii[:], in1=HH[:], op=ALU.mult), r=["ii", "HH"], w=["rnnT"])
                for k in range(4):
                    op("pe", lambda h, k=k: h.transpose(pT[:, k * 128:(k + 1) * 128], rnnT[:, k, :], identb[:]), r=["rnnT", "identb"], w=["pT"])
                op("act", lambda h: h.activation(out=catR[:], in_=pT[:, 0:512], func=AF.Identity), r=["pT"], w=["catR"])
                dma("sp", lambda h: h.dma_start(out=catd[i * 128:(i + 1) * 128, 512:1024], in_=catR[:]), r=["catR"], w=[("catd", i, 1)])

            if stage >= 1:
                stage_A(0)
                stage_I(0)
                stage_R1(0)
                for i in range(NT):
                    if i + 1 < NT:
                        stage_A(i + 1)
                    stage_T(i)
                    if i > 0:
                        stage_AT(i - 1)
                    stage_M(i)
                    stage_R2(i)
                    if i + 1 < NT:
                        stage_I(i + 1)
                        stage_R1(i + 1)
                    if i > 0:
                        stage_ATn(i - 1)
                stage_AT(NT - 1)
                stage_ATn(NT - 1)

            if with_sample:
                scS0 = sb("scS0", [128, 4, 12]); h0 = sb("h0", [128, 4, 4])
                dma("sp", lambda h: h.dma_start(out=xin[0:4, :], in_=xs), w=["xin"])
                for b in range(4):
                    for j in range(3):
                        dma("sp", lambda h, b=b, j=j: h.dma_start(out=scS0[:, :, j * 4 + b], in_=scv[b, j].rearrange("(k p) -> p k", p=128)), w=["scS0"])
                    dma("sp", lambda h, b=b: h.dma_start(out=h0[:, :, b], in_=sh[b].rearrange("(k p) -> p k", p=128)), w=["h0"])
                op("act", lambda h: h.activation(out=xb[0:4, :], in_=xin[0:4, :], func=AF.Identity), r=["xin"], w=["xb"])
                for c in range(8):
                    op("pe", lambda h, c=c: h.transpose(pT[:, c * 128:c * 128 + 4], xb[0:4, c * 128:(c + 1) * 128], identb[0:4, 0:4]), r=["xb", "identb"], w=["pT"])
                op("dve", lambda h: h.tensor_copy(out=xT[:, :, 0:4], in_=pT[:, :].rearrange("p (c t) -> p c t", c=8)[:, :, 0:4]), r=["pT"], w=["xT"])
                grp = [(0, 512, kst, "kst", None), (512, 512, vst, "vst", ksm), (1024, 512, kst, "kst", vsm), (1536, 512, vst, "vst", None), (2048, 72, kist, "kist", kis)]
                for gi, (c0, n, stg, sk, outd) in enumerate(grp):
                    pb_, pk_ = (pA, "pA") if gi % 2 == 0 else (pB, "pB")
                    for c in range(8):
                        op("pe", lambda h, c=c, pb_=pb_, c0=c0, n=n: h.matmul(pb_[0:4, 0:n], lhsT=xT[:, c, 0:4], rhs=win[:, c, c0:c0 + n], start=(c == 0), stop=(c == 7)), r=["win", "xT"], w=[pk_])
                    op("act", lambda h, pb_=pb_, stg=stg, n=n: h.activation(out=stg[0:4, 0:n], in_=pb_[0:4, 0:n], func=AF.Identity), r=[pk_], w=[sk])
                    dma("sp", lambda h, stg=stg, c0=c0, n=n: h.dma_start(out=sst[:, c0:c0 + n], in_=stg[0:4, 0:n]), r=[sk], w=["sst"])
                    if outd is not None:
                        dma("sp", lambda h, stg=stg, outd=outd: h.dma_start(out=outd, in_=stg[0:4, 0:outd.shape[1]]), r=[sk])
                for (c0, pb_, pk_) in ((2120, pA, "pA"), (2632, pB, "pB")):
                    for k in range(4):
                        for c in range(8):
                            op("pe", lambda h, k=k, c=c, pb_=pb_, c0=c0: h.matmul(pb_[:, k * 128:k * 128 + 4], lhsT=win[:, c, c0 + k * 128:c0 + (k + 1) * 128], rhs=xT[:, c, 0:4], start=(c == 0), stop=(c == 7)),
                               r=["win", "xT"], w=[pk_])
                op("act", lambda h: h.activation(out=XR[:, :, 3:7], in_=pA[:, :].rearrange("p (k t) -> p k t", k=4)[:, :, 0:4], func=AF.Identity), r=["pA"], w=["XR"])
                op("dve", lambda h: h.tensor_copy(out=GR[:, :, 0:4], in_=pB[:, :].rearrange("p (k t) -> p k t", k=4)[:, :, 0:4]), r=["pB"], w=["GR"])
                for k in range(4):
                    op("dve", lambda h, k=k: h.tensor_scalar(out=xc[:, k, 0:4], in0=XR[:, k, 3:7], scalar1=cw[:, k, 3:4], scalar2=cb[:, k:k + 1], op0=ALU.mult, op1=ALU.add), r=["XR", "cw", "cb"], w=["xc"])
                    for j in range(3):
                        op("dve", lambda h, k=k, j=j: h.scalar_tensor_tensor(out=xc[:, k, 0:4], in0=scS0[:, k, j * 4:(j + 1) * 4], scalar=cw[:, k, j:j + 1], in1=xc[:, k, 0:4], op0=ALU.mult, op1=ALU.add),
                           r=["scS0", "cw", "xc"], w=["xc"])
                for b in range(4):
                    for j in range(2):
                        dma("sp", lambda h, b=b, j=j: h.dma_start(out=cs[b, j].rearrange("(k p) -> p k", p=128), in_=scS0[:, :, (j + 1) * 4 + b]), r=["scS0"])
                    dma("sp", lambda h, b=b: h.dma_start(out=cs[b, 2].rearrange("(k p) -> p k", p=128), in_=XR[:, :, 3 + b]), r=["XR"])
                for k in range(4):
                    op("pe", lambda h, k=k: h.matmul(pA[:, k * 128:k * 128 + 4], lhsT=WA[:, k, :], rhs=xc[:, k, 0:4], start=True, stop=True), r=["WA", "xc"], w=["pA"])
                    op("pe", lambda h, k=k: h.matmul(pB[:, k * 128:k * 128 + 4], lhsT=WX[:, k, :], rhs=xc[:, k, 0:4], start=True, stop=True), r=["WX", "xc"], w=["pB"])
                for k in range(4):
                    op("act", lambda h, k=k: h.activation(out=rr[:, k, 0:4], in_=pA[:, k * 128:k * 128 + 4], func=AF.Sigmoid, bias=ba[:, k:k + 1]), r=["pA", "ba"], w=["rr"])
                    op("act", lambda h, k=k: h.activation(out=ii[:, k, 0:4], in_=pB[:, k * 128:k * 128 + 4], func=AF.Sigmoid, bias=bx[:, k:k + 1]), r=["pB", "bx"], w=["ii"])
                for k in range(4):
                    op("act", lambda h, k=k: h.activation(out=rr[:, k, 0:4], in_=rr[:, k, 0:4], func=AF.Exp, scale=c8[:, k:k + 1]), r=["rr", "c8"], w=["rr"])
                op("pool", lambda h: h.tensor_tensor(out=uu[:, :, 0:4], in0=rr[:, :, 0:4], in1=rr[:, :, 0:4], op=ALU.mult), r=["rr"], w=["uu"])
                op("act", lambda h: h.activation(out=uu[:, :, 0:4], in_=uu[:, :, 0:4], func=AF.Sqrt, scale=-1.0, bias=1.0), r=["uu"], w=["uu"])
                op("pool", lambda h: h.tensor_tensor(out=uu[:, :, 0:4], in0=uu[:, :, 0:4], in1=ii[:, :, 0:4], op=ALU.mult), r=["uu", "ii"], w=["uu"])
                op("pool", lambda h: h.tensor_tensor(out=uu[:, :, 0:4], in0=uu[:, :, 0:4], in1=xc[:, :, 0:4], op=ALU.mult), r=["uu", "xc"], w=["uu"])
                op("pool", lambda h: h.tensor_tensor(out=HH[:, :, 0:4], in0=rr[:, :, 0:4], in1=h0[:], op=ALU.mult), r=["rr", "h0"], w=["HH"])
                op("pool", lambda h: h.tensor_tensor(out=HH[:, :, 0:4], in0=HH[:, :, 0:4], in1=uu[:, :, 0:4], op=ALU.add), r=["HH", "uu"], w=["HH"])
                for b in range(4):
                    dma("sp", lambda h, b=b: h.dma_start(out=hs[b].rearrange("(k p) -> p k", p=128), in_=HH[:, :, b]), r=["HH"])
                t4 = ii[:, :, 0:4]; g4 = GR[:, :, 0:4]
                op("pool", lambda h: h.tensor_tensor(out=t4, in0=g4, in1=g4, op=ALU.mult), r=["GR"], w=["ii"])
                op("pool", lambda h: h.tensor_scalar(out=t4, in0=t4, scalar1=0.044715, scalar2=1.0, op0=ALU.mult, op1=ALU.add), r=["ii"], w=["ii"])
                op("pool", lambda h: h.tensor_tensor(out=t4, in0=t4, in1=g4, op=ALU.mult), r=["ii", "GR"], w=["ii"])
                op("act", lambda h: h.activation(out=t4, in_=t4, func=AF.Sigmoid, scale=1.5957691216057308), r=["ii"], w=["ii"])
                op("pool", lambda h: h.tensor_tensor(out=t4, in0=t4, in1=g4, op=ALU.mult), r=["ii", "GR"], w=["ii"])
                op("pool", lambda h: h.tensor_tensor(out=rnnT[:, :, 0:4], in0=t4, in1=HH[:, :, 0:4], op=ALU.mult), r=["ii", "HH"], w=["rnnT"])
                for k in range(4):
                    op("pe", lambda h, k=k: h.transpose(pT[0:4, k * 128:(k + 1) * 128], rnnT[:, k, 0:4], identb[:]), r=["rnnT", "identb"], w=["pT"])
                op("act", lambda h: h.activation(out=catR[0:4, :], in_=pT[0:4, 0:512], func=AF.Identity), r=["pT"], w=["catR"])
                dma("sp", lambda h: h.dma_start(out=catd[NT * 128:NT * 128 + 4, 512:1024], in_=catR[0:4, :]), r=["catR"], w=[("catd", NT, 1)])
            sc_.flush()

        if with_sample:
          with ExitStack() as es:
            def sb(n, s, d=F32):
                return es.enter_context(nc.sbuf_tensor(n, s, d))

            def ps(n, s, d=F32):
                return es.enter_context(nc.psum_tensor(n, s, d))
            NP_ = NPG
            kig = sb("kig", [128, 8192])
            KITo = [sb("KITo%d" % i, [128, 4, 128], BF16) for i in range(2)]
            identf = sb("identf", [128, 128]); onesf = sb("onesf", [128, 128]); sut = sb("sut", [128, 128])
            iota256 = sb("iota256", [128, 256]); iota32 = sb("iota32", [128, 32]); bthr = sb("bthr", [128, 31])
            selfm = sb("selfm", [128, 1]); posc = sb("posc", [128, 129]); offc = sb("offc", [128, 129])
            ones129 = sb("ones129", [128, 129])
            selr = sb("selr", [4, 4, 128]); selc = sb("selc", [128, 4, 4])
            sq = sb("sq", [4, 2120]); qiTs = sb("qiTs", [64, 8, 4]); qiTb = sb("qiTb", [64, 8, 4], BF16)
            ptcol = sb("ptcol", [128, 4], I32); pt128 = sb("pt128", [128, 4])
            wbc = sb("wbc", [128, 4, 8]); Rb = sb("Rb", [128, 512])
            scS = sb("scS", [128, 4, 130])
            pq = sb("pq", [4, 8, 64]); dself = sb("dself", [4, 8]); sself = sb("sself", [4, 1])
            los = sb("los", [128, 4]); mids = sb("mids", [128, 4]); cps = sb("cps", [128, 4]); ggs = sb("ggs", [128, 4]); junks = sb("junks", [128, 130], U8)
            mS = sb("mS", [128, 129]); cum = sb("cum", [128, 129]); slot = sb("slot", [128, 129])
            vals = sb("vals", [128, 129, 3]); OHt = [sb("OHt%d" % i, [128, 256]) for i in range(2)]
            gsel = sb("gsel", [128, 2, 3]); idx32 = sb("idx32", [128, 2], I32)
            Ksel = sb("Ksel", [128, 2, 512]); Vsel = sb("Vsel", [128, 2, 512])
            qbc = sb("qbc", [128, 512]); kbc = sb("kbc", [128, 512]); vbc = sb("vbc", [128, 512]); prod = sb("prod", [128, 512])
            lg = sb("lg", [128, 2, 8]); lgself = sb("lgself", [128, 8]); dlt = sb("dlt", [128, 8])
            isself = sb("isself", [128, 2]); dist = sb("dist", [128, 2])
            ge = sb("ge", [128, 31]); bk = sb("bk", [128, 1]); OHB = sb("OHB", [128, 32])
            rbT1 = sb("rbT1", [1, 256]); rbbc = sb("rbbc", [128, 8, 32]); pbias = sb("pbias", [128, 8, 32]); bias_t = sb("bias_t", [128, 2, 8])
            pp = sb("pp", [128, 2, 8]); pv = sb("pv", [128, 2, 512])
            rd = sb("rd", [4, 8]); cats = sb("cats", [4, 512], BF16)

            pTf = ps("pTf", [128, 512]); psc = [ps("psc%d" % i, [128, 512]) for i in range(2)]
            pX = ps("pX", [128, 512]); pG = [ps("pG%d" % i, [128, 512]) for i in range(2)]
            pN = ps("pN", [128, 512]); pD = ps("pD", [128, 512])

            for nm, tl in (("ident", identf), ("ones", onesf), ("sut", sut), ("iota256", iota256), ("iota32", iota32), ("bthr", bthr), ("selfm", selfm), ("posc", posc), ("offc", offc)):
                dma("sp", lambda h, nm=nm, tl=tl: h.dma_start(out=tl[:], in_=C[nm]), w=[nm + "_s"])
            op("dve", lambda h: h.memset(ones129[:], 1.0), w=["ones129"])
            op("dve", lambda h: h.tensor_copy(out=selr[:], in_=identf[0:4, 0:4].unsqueeze(2).to_broadcast([4, 4, 128])), r=["ident_s"], w=["selr"])
            op("dve", lambda h: h.memset(selc[:], 0.0), w=["selc"])
            for b in range(4):
                op("dve", lambda h, b=b: h.memset(selc[:, b, b:b + 1], 1.0), w=["selc"])
            dma("sp", lambda h: h.dma_start(out=sq[:], in_=sst), r=["sst"], w=["sq"])
            for b in range(4):
                dma("sp", lambda h, b=b: h.dma_start(out=qiTs[:, :, b], in_=sst[b, 1536:2048].rearrange("(a p) -> p a", p=64)), r=["sst"], w=["qiTs"])
                dma("sp", lambda h, b=b: h.dma_start(out=ptcol[0:NP_, b:b + 1], in_=pt[b, :].rearrange("(p o) -> p o", o=1)), w=["ptcol"])
            op("dve", lambda h: h.tensor_copy(out=qiTb[:], in_=qiTs[:]), r=["qiTs"], w=["qiTb"])
            op("dve", lambda h: h.memset(pt128[:], 0.0), w=["pt128"])
            op("dve", lambda h: h.tensor_copy(out=pt128[0:NP_, :], in_=ptcol[0:NP_, :]), r=["ptcol"], w=["pt128"])
            op("dve", lambda h: h.tensor_scalar(out=pt128[0:NP_, :], in0=pt128[0:NP_, :], scalar1=128.0, scalar2=None, op0=ALU.mult), r=["pt128"], w=["pt128"])
            dma("sp", lambda h: h.dma_start(out=rbT1[0:1, :].rearrange("o (a b) -> o a b", a=8), in_=relb.rearrange("(o b) a -> o a b", o=1)), w=["rbT1"])
            op("pe", lambda h: h.matmul(pX[:, 0:256], lhsT=onesf[0:1, :], rhs=rbT1[0:1, :], start=True, stop=True), r=["ones_s", "rbT1"], w=["pX"])
            op("act", lambda h: h.activation(out=rbbc[:].rearrange("p a b -> p (a b)"), in_=pX[:, 0:256], func=AF.Identity), r=["pX"], w=["rbbc"])
            op("dve", lambda h: h.memset(scS[:], NEG), w=["scS"])

            def bcast(dst_ps, b, rhs_ap, n, rk):
                op("pe", lambda h: h.matmul(dst_ps[:, 0:n], lhsT=selr[0:4, b, :], rhs=rhs_ap, start=True, stop=True), r=["selr"] + rk, w=["pX"])

            for b in range(4):
                bcast(pX, b, sq[0:4, 2112:2120], 8, ["sq"])
                op("dve", lambda h, b=b: h.tensor_copy(out=wbc[:, b, :], in_=pX[:, 0:8]), r=["pX"], w=["wbc"])
            op("dve", lambda h: h.tensor_tensor(out=pq[:], in0=sq[0:4, 1536:2048].rearrange("p (a d) -> p a d", a=8), in1=sq[0:4, 2048:2112].unsqueeze(1).to_broadcast([4, 8, 64]), op=ALU.mult), r=["sq"], w=["pq"])
            op("dve", lambda h: h.tensor_reduce(out=dself[:], in_=pq[:], axis=AX.X, op=ALU.add), r=["pq"], w=["dself"])
            op("dve", lambda h: h.tensor_scalar(out=dself[:], in0=dself[:], scalar1=0.0, scalar2=None, op0=ALU.max), r=["dself"], w=["dself"])
            op("dve", lambda h: h.tensor_tensor(out=dself[:], in0=dself[:], in1=sq[0:4, 2112:2120], op=ALU.mult), r=["dself", "sq"], w=["dself"])
            op("dve", lambda h: h.tensor_reduce(out=sself[:], in_=dself[:], axis=AX.X, op=ALU.add), r=["dself"], w=["sself"])
            for b in range(4):
                bcast(pX, b, sself[0:4, 0:1], 1, ["sself"])
                op("dve", lambda h, b=b: h.tensor_scalar(out=scS[:, b, 128:129], in0=pX[:, 0:1], scalar1=selfm[:, 0:1], scalar2=None, op0=ALU.add), r=["pX", "selfm_s"], w=["scS"])
            ko = 0
            for b in range(4):
                dma("pool", lambda h, b=b: h.indirect_dma_start(out=kig[0:NP_, :], out_offset=None, in_=cki.rearrange("(n p) d -> n (p d)", p=128),
                                                              in_offset=bass.IndirectOffsetOnAxis(ap=ptcol[0:NP_, b:b + 1], axis=0)), r=["ptcol"], w=["kig"])
                for half in range(2):
                    for og in range(16):
                        kt = KITo[ko % 2]; ktk = "KITo%d" % (ko % 2); ko += 1
                        for o4 in range(4):
                            o = half * 64 + og * 4 + o4
                            op("pe", lambda h, o=o, o4=o4: h.transpose(pTf[0:64, o4 * 128:o4 * 128 + NP_], kig[0:NP_, o * 64:(o + 1) * 64], identf[0:NP_, 0:NP_]), r=["kig", "ident_s"], w=["pTf"])
                        ev = "act" if og % 2 else "dve"
                        if ev == "act":
                            op("act", lambda h, kt=kt: h.activation(out=kt[0:64, :, 0:NP_], in_=pTf[0:64, :].rearrange("p (a t) -> p a t", a=4)[:, :, 0:NP_], func=AF.Identity), r=["pTf"], w=[ktk])
                        else:
                            op("dve", lambda h, kt=kt: h.tensor_copy(out=kt[0:64, :, 0:NP_], in_=pTf[0:64, :].rearrange("p (a t) -> p a t", a=4)[:, :, 0:NP_]), r=["pTf"], w=[ktk])
                        for o4 in range(4):
                            ol = og * 4 + o4
                            op("pe", lambda h, kt=kt, o4=o4, ol=ol, half=half, b=b: h.matmul(psc[half][0:NP_, ol * 8:(ol + 1) * 8], lhsT=kt[0:64, o4, 0:NP_], rhs=qiTb[0:64, :, b], start=True, stop=True),
                               r=[ktk, "qiTb"], w=["psc%d" % half])
                    op("act", lambda h, half=half: h.activation(out=Rb[0:NP_, :], in_=psc[half][0:NP_, :], func=AF.Relu), r=["psc%d" % half], w=["Rb"])
                    op("dve", lambda h, b=b: h.tensor_tensor(out=Rb[0:NP_, :].rearrange("p (o a) -> p o a", a=8), in0=Rb[0:NP_, :].rearrange("p (o a) -> p o a", a=8),
                                                            in1=wbc[0:NP_, b, :].unsqueeze(1).to_broadcast([NP_, 64, 8]), op=ALU.mult), r=["Rb", "wbc"], w=["Rb"])
                    op("dve", lambda h, b=b, half=half: h.tensor_reduce(out=scS[0:NP_, b, half * 64:(half + 1) * 64], in_=Rb[0:NP_, :].rearrange("p (o a) -> p o a", a=8), axis=AX.X, op=ALU.add),
                       r=["Rb"], w=["scS"])
            op("dve", lambda h: h.memset(los[:], -1024.0), w=["los"])
            for it in range(NIT):
                hk = 1024.0 / (2.0 ** it)
                op("dve", lambda h, hk=hk: h.tensor_scalar(out=mids[:], in0=los[:], scalar1=hk, scalar2=None, op0=ALU.add), r=["los"], w=["mids"])
                for b in range(4):
                    op("dve", lambda h, b=b: h.tensor_scalar(out=junks[:, 0:129], in0=scS[:, b, 0:129], scalar1=mids[:, b:b + 1], scalar2=None, op0=ALU.is_gt, op1=ALU.add, accum_out=cps[:, b:b + 1]),
                       r=["scS", "mids"], w=["junks", "cps"])
                op("pe", lambda h: h.matmul(pX[:, 0:4], lhsT=onesf[:, :], rhs=cps[:, 0:4], start=True, stop=True), r=["ones_s", "cps"], w=["pX"])
                op("dve", lambda h, hk=hk: h.tensor_scalar(out=ggs[:], in0=pX[:, 0:4], scalar1=float(TOPK_S), scalar2=hk, op0=ALU.is_ge, op1=ALU.mult), r=["pX"], w=["ggs"])
                op("dve", lambda h: h.tensor_tensor(out=los[:], in0=los[:], in1=ggs[:], op=ALU.add), r=["los", "ggs"], w=["los"])
            op("dve", lambda h: h.tensor_copy(out=vals[:, :, 1], in_=posc[:]), r=["posc_s"], w=["vals"])
            op("dve", lambda h: h.memset(vals[:, :, 2], 1.0), w=["vals"])
            oi = 0
            for b in range(4):
                op("dve", lambda h, b=b: h.tensor_scalar(out=mS[:], in0=scS[:, b, 0:129], scalar1=los[:, b:b + 1], scalar2=None, op0=ALU.is_gt), r=["scS", "los"], w=["mS"])
                op("dve", lambda h: h.tensor_tensor_scan(out=cum[:], data0=ones129[:], data1=mS[:], initial=0.0, op0=ALU.mult, op1=ALU.add), r=["ones129", "mS"], w=["cum"])
                op("pe", lambda h: h.matmul(pX[:, 0:1], lhsT=sut[:, :], rhs=cum[:, 128:129], start=True, stop=True), r=["sut_s", "cum"], w=["pX"])
                op("dve", lambda h: h.tensor_scalar(out=slot[:], in0=cum[:], scalar1=pX[:, 0:1], scalar2=None, op0=ALU.add), r=["cum", "pX"], w=["slot"])
                op("dve", lambda h: h.tensor_tensor(out=slot[:], in0=slot[:], in1=mS[:], op=ALU.mult), r=["slot", "mS"], w=["slot"])
                op("dve", lambda h: h.tensor_scalar(out=slot[:], in0=slot[:], scalar1=-1.0, scalar2=None, op0=ALU.add), r=["slot"], w=["slot"])
                op("dve", lambda h, b=b: h.tensor_scalar(out=vals[:, :, 0], in0=offc[:], scalar1=pt128[:, b:b + 1], scalar2=None, op0=ALU.add), r=["offc_s", "pt128"], w=["vals"])
                op("dve", lambda h: h.memset(vals[:, 128, 0:1], 0.0), w=["vals"])
                for o in range(129):
                    oh = OHt[oi % 2]; ohk = "OHt%d" % (oi % 2); oi += 1
                    op("dve", lambda h, oh=oh, o=o: h.tensor_scalar(out=oh[:], in0=iota256[:], scalar1=slot[:, o:o + 1], scalar2=None, op0=ALU.is_equal), r=["iota256_s", "slot"], w=[ohk])
                    for half in range(2):
                        op("pe", lambda h, oh=oh, o=o, half=half: h.matmul(pG[half][:, 0:3], lhsT=oh[:, half * 128:(half + 1) * 128], rhs=vals[:, o, :], start=(o == 0), stop=(o == 128)),
                           r=[ohk, "vals"], w=["pG%d" % half])
                for half in range(2):
                    op("dve", lambda h, half=half: h.tensor_copy(out=gsel[:, half, :], in_=pG[half][:, 0:3]), r=["pG%d" % half], w=["gsel"])
                op("dve", lambda h: h.tensor_copy(out=idx32[:], in_=gsel[:, :, 0]), r=["gsel"], w=["idx32"])
                for half in range(2):
                    dma("pool", lambda h, half=half: h.indirect_dma_start(out=Ksel[:, half, :], out_offset=None, in_=ck, in_offset=bass.IndirectOffsetOnAxis(ap=idx32[:, half:half + 1], axis=0)),
                        r=["idx32"], w=["Ksel"])
                    dma("pool", lambda h, half=half: h.indirect_dma_start(out=Vsel[:, half, :], out_offset=None, in_=cv, in_offset=bass.IndirectOffsetOnAxis(ap=idx32[:, half:half + 1], axis=0)),
                        r=["idx32"], w=["Vsel"])
                for (c0, dst, dk) in ((0, qbc, "qbc"), (512, kbc, "kbc"), (1024, vbc, "vbc")):
                    bcast(pX, b, sq[0:4, c0:c0 + 512], 512, ["sq"])
                    op("act", lambda h, dst=dst: h.activation(out=dst[:], in_=pX[:, :], func=AF.Identity), r=["pX"], w=[dk])
                op("dve", lambda h: h.tensor_scalar(out=isself[:], in0=gsel[:, :, 1], scalar1=float(PAST), scalar2=None, op0=ALU.is_equal), r=["gsel"], w=["isself"])
                op("dve", lambda h: h.tensor_scalar(out=dist[:], in0=gsel[:, :, 1], scalar1=-1.0, scalar2=float(PAST), op0=ALU.mult, op1=ALU.add), r=["gsel"], w=["dist"])
                op("dve", lambda h: h.tensor_tensor(out=prod[:], in0=kbc[:], in1=qbc[:], op=ALU.mult), r=["kbc", "qbc"], w=["prod"])
                op("dve", lambda h: h.tensor_reduce(out=lgself[:], in_=prod[:].rearrange("p (a d) -> p a d", a=8), axis=AX.X, op=ALU.add), r=["prod"], w=["lgself"])
                for half in range(2):
                    op("dve", lambda h, half=half: h.tensor_tensor(out=prod[:], in0=Ksel[:, half, :], in1=qbc[:], op=ALU.mult), r=["Ksel", "qbc"], w=["prod"])
                    op("dve", lambda h, half=half: h.tensor_reduce(out=lg[:, half, :], in_=prod[:].rearrange("p (a d) -> p a d", a=8), axis=AX.X, op=ALU.add), r=["prod"], w=["lg"])
                    op("dve", lambda h, half=half: h.tensor_tensor(out=dlt[:], in0=lgself[:], in1=lg[:, half, :], op=ALU.subtract), r=["lgself", "lg"], w=["dlt"])
                    op("dve", lambda h, half=half: h.scalar_tensor_tensor(out=lg[:, half, :], in0=dlt[:], scalar=isself[:, half:half + 1], in1=lg[:, half, :], op0=ALU.mult, op1=ALU.add),
                       r=["dlt", "isself", "lg"], w=["lg"])
                    op("dve", lambda h, half=half: h.tensor_scalar(out=ge[:], in0=bthr[:], scalar1=dist[:, half:half + 1], scalar2=None, op0=ALU.is_le), r=["bthr_s", "dist"], w=["ge"])
                    op("dve", lambda h: h.tensor_reduce(out=bk[:], in_=ge[:], axis=AX.X, op=ALU.add), r=["ge"], w=["bk"])
                    op("dve", lambda h: h.tensor_scalar(out=OHB[:], in0=iota32[:], scalar1=bk[:, 0:1], scalar2=None, op0=ALU.is_equal), r=["iota32_s", "bk"], w=["OHB"])
                    op("dve", lambda h: h.tensor_tensor(out=pbias[:], in0=rbbc[:], in1=OHB[:].unsqueeze(1).to_broadcast([128, 8, 32]), op=ALU.mult), r=["rbbc", "OHB"], w=["pbias"])
                    op("dve", lambda h, half=half: h.tensor_reduce(out=bias_t[:, half, :], in_=pbias[:], axis=AX.X, op=ALU.add), r=["pbias"], w=["bias_t"])
                    op("pool", lambda h, half=half: h.tensor_tensor(out=prod[:], in0=vbc[:], in1=Vsel[:, half, :], op=ALU.subtract), r=["vbc", "Vsel", "lg"], w=["prod"])
                    op("dve", lambda h, half=half: h.scalar_tensor_tensor(out=Vsel[:, half, :], in0=prod[:], scalar=isself[:, half:half + 1], in1=Vsel[:, half, :], op0=ALU.mult, op1=ALU.add),
                       r=["prod", "isself", "Vsel"], w=["Vsel"])
                op("dve", lambda h: h.scalar_tensor_tensor(out=lg[:], in0=lg[:], scalar=0.125, in1=bias_t[:], op0=ALU.mult, op1=ALU.add), r=["lg", "bias_t"], w=["lg"])
                op("act", lambda h: h.activation(out=pp[:], in_=lg[:], func=AF.Exp), r=["lg"], w=["pp"])
                op("dve", lambda h: h.tensor_tensor(out=pp[:], in0=pp[:], in1=gsel[:, :, 2:3].to_broadcast([128, 2, 8]), op=ALU.mult), r=["pp", "gsel"], w=["pp"])
                for half in range(2):
                    op("dve", lambda h, half=half: h.tensor_tensor(out=pv[:, half, :].rearrange("p (a d) -> p a d", a=8), in0=Vsel[:, half, :].rearrange("p (a d) -> p a d", a=8),
                                                                in1=pp[:, half, :].unsqueeze(2).to_broadcast([128, 8, 64]), op=ALU.mult), r=["Vsel", "pp"], w=["pv"])
                for half in range(2):
                    st_ = (b == 0 and half == 0); sp_ = (b == 3 and half == 1)
                    op("pe", lambda h, b=b, half=half, st_=st_, sp_=sp_: h.matmul(pN[0:4, 0:512], lhsT=selc[:, b, :], rhs=pv[:, half, :], start=st_, stop=sp_), r=["selc", "pv"], w=["pN"])
                    op("pe", lambda h, b=b, half=half, st_=st_, sp_=sp_: h.matmul(pD[0:4, 0:8], lhsT=selc[:, b, :], rhs=pp[:, half, :], start=st_, stop=sp_), r=["selc", "pp"], w=["pD"])
            op("dve", lambda h: h.reciprocal(out=rd[:], in_=pD[0:4, 0:8]), r=["pD"], w=["rd"])
            op("dve", lambda h: h.tensor_tensor(out=cats[:].rearrange("p (a d) -> p a d", a=8), in0=pN[0:4, 0:512].rearrange("p (a d) -> p a d", a=8), in1=rd[:].unsqueeze(2).to_broadcast([4, 8, 64]), op=ALU.mult),
               r=["pN", "rd"], w=["cats"])
            dma("sp", lambda h: h.dma_start(out=catd[NT * 128:NT * 128 + 4, 0:512], in_=cats[:]), r=["cats"], w=[("catd", NT, 0)])
            sc_.flush()

        if with_passB:
          with ExitStack() as es:
            def sb(n, s, d=F32):
                return es.enter_context(nc.sbuf_tensor(n, s, d))

            def ps(n, s, d=F32):
                return es.enter_context(nc.psum_tensor(n, s, d))
            wout = sb("wout", [128, 8, D], BF16)
            wup = sb("wup", [128, 8, DFF], BF16)
            wdn = sb("wdn", [128, 32, D], BF16)
            G1 = sb("G1", [128, D]); B1 = sb("B1", [128, D]); G2 = sb("G2", [128, D]); B2 = sb("B2", [128, D]); BD = sb("BD", [128, D])
            bup = sb("bup", [128, 32])
            catb = sb("catb", [128, D], BF16)
            catT = sb("catT", [128, 8, 128], BF16)
            xin2 = sb("xin2", [128, D])
            x1 = sb("x1", [128, D]); x1b = sb("x1b", [128, D], BF16); x1T = sb("x1T", [128, 8, 128], BF16)
            hidT = sb("hidT", [128, 32, 128], BF16)
            rl = [sb("rl%d" % i, [128, 512]) for i in range(2)]
            yt = sb("yt", [128, D])
            st = sb("st", [128, 2, 6]); mv = sb("mv", [128, 2]); sd = sb("sd", [128, 1])
            pM = [ps("pM%d" % i, [128, 512]) for i in range(2)]
            pU = [ps("pU%d" % i, [128, 512]) for i in range(2)]
            pT2 = ps("pT2", [128, 1024], BF16)

            wo_v = w_out.rearrange("(c p) n -> p c n", p=128)
            wu_v = w_up.rearrange("(c p) n -> p c n", p=128)
            wd_v = w_down.rearrange("(c p) n -> p c n", p=128)
            for c in range(8):
                dma("pool", lambda h, c=c: h.dma_start(out=wout[:, c, :], in_=wo_v[:, c, :]), w=["wout"])
            for c in range(8):
                for hf in range(2):
                    dma("pool", lambda h, c=c, hf=hf: h.dma_start(out=wup[:, c, hf * 2048:(hf + 1) * 2048], in_=wu_v[:, c, hf * 2048:(hf + 1) * 2048]), w=["wup"])
            for c in range(32):
                dma("pool", lambda h, c=c: h.dma_start(out=wdn[:, c, :], in_=wd_v[:, c, :]), w=["wdn"])
            for nm, tl, src in (("G1", G1, ln1_g), ("B1", B1, ln1_b), ("G2", G2, ln2_g), ("B2", B2, ln2_b), ("BD", BD, b_down)):
                dma("sp", lambda h, tl=tl, src=src: h.dma_start(out=tl[:], in_=src.unsqueeze(0).to_broadcast([128, D])), w=[nm])
            dma("sp", lambda h: h.dma_start(out=bup[:], in_=b_up.rearrange("(f p) -> p f", p=128)), w=["bup"])

            def layer_norm(src, dst, Gt, Bt, gk, bk, R, srck, dstk):
                for hf in range(2):
                    op("dve", lambda h, hf=hf: h.bn_stats(out=st[0:R, hf, :], in_=src[0:R, hf * 512:(hf + 1) * 512]), r=[srck], w=["st"])
                op("dve", lambda h: h.bn_aggr(out=mv[0:R, :], in_=st[0:R, :, :].rearrange("p a b -> p (a b)")), r=["st"], w=["mv"])
                op("act", lambda h: h.activation(out=sd[0:R, :], in_=mv[0:R, 1:2], func=AF.Sqrt, bias=EPS), r=["mv"], w=["sd"])
                op("dve", lambda h: h.reciprocal(out=sd[0:R, :], in_=sd[0:R, :]), r=["sd"], w=["sd"])
                op("dve", lambda h: h.tensor_scalar(out=dst[0:R, :], in0=src[0:R, :], scalar1=mv[0:R, 0:1], scalar2=sd[0:R, 0:1], op0=ALU.subtract, op1=ALU.mult),
                   r=[srck, "mv", "sd"], w=[dstk])
                op("dve", lambda h: h.tensor_tensor(out=dst[0:R, :], in0=dst[0:R, :], in1=Gt[0:R, :], op=ALU.mult), r=[dstk, gk], w=[dstk])
                op("pool", lambda h: h.tensor_tensor(out=dst[0:R, :], in0=dst[0:R, :], in1=Bt[0:R, :], op=ALU.add), r=[dstk, bk], w=[dstk])

            tiles = list(range(NT)) + ([NT] if with_sample else [])
            for i in tiles:
                R = 128 if i < NT else 4
                xsrc = x[i * 128:(i + 1) * 128, :] if i < NT else xs
                ydst = y[i * 128:(i + 1) * 128, :] if i < NT else ys
                dma("sp", lambda h, i=i, R=R: h.dma_start(out=catb[0:R, :], in_=catd[i * 128:i * 128 + R, :]), r=[("catd", i, 0), ("catd", i, 1)], w=["catb"])
                dma("sp", lambda h, R=R, xsrc=xsrc: h.dma_start(out=xin2[0:R, :], in_=xsrc), w=["xin2"])
                for c in range(8):
                    op("pe", lambda h, c=c, R=R: h.transpose(pT2[:, c * 128:c * 128 + R], catb[0:R, c * 128:(c + 1) * 128], identb[0:R, 0:R]), r=["catb", "identb"], w=["pT2"])
                op("act", lambda h, R=R: h.activation(out=catT[:, :, 0:R], in_=pT2[:, :].rearrange("p (c t) -> p c t", c=8)[:, :, 0:R], func=AF.Identity), r=["pT2"], w=["catT"])
                for hf in range(2):
                    for c in range(8):
                        op("pe", lambda h, hf=hf, c=c, R=R: h.matmul(pM[hf][0:R, :], lhsT=catT[:, c, 0:R], rhs=wout[:, c, hf * 512:(hf + 1) * 512], start=(c == 0), stop=(c == 7)),
                           r=["catT", "wout"], w=["pM%d" % hf])
                for hf in range(2):
                    op("dve", lambda h, hf=hf, R=R: h.scalar_tensor_tensor(out=yt[0:R, hf * 512:(hf + 1) * 512], in0=xin2[0:R, hf * 512:(hf + 1) * 512], scalar=ALPHA, in1=pM[hf][0:R, :],
                                                                      op0=ALU.mult, op1=ALU.add), r=["xin2", "pM%d" % hf], w=["yt"])
                layer_norm(yt, x1, G1, B1, "G1", "B1", R, "yt", "x1")
                op("act", lambda h, R=R: h.activation(out=x1b[0:R, :], in_=x1[0:R, :], func=AF.Identity), r=["x1"], w=["x1b"])
                for c in range(8):
                    op("pe", lambda h, c=c, R=R: h.transpose(pT2[:, c * 128:c * 128 + R], x1b[0:R, c * 128:(c + 1) * 128], identb[0:R, 0:R]), r=["x1b", "identb"], w=["pT2"])
                op("dve", lambda h, R=R: h.tensor_copy(out=x1T[:, :, 0:R], in_=pT2[:, :].rearrange("p (c t) -> p c t", c=8)[:, :, 0:R]), r=["pT2"], w=["x1T"])
                for fg in range(8):
                    pu = pU[fg % 2]; puk = "pU%d" % (fg % 2); rlt = rl[fg % 2]; rlk = "rl%d" % (fg % 2)
                    for f4 in range(4):
                        f = fg * 4 + f4
                        for c in range(8):
                            op("pe", lambda h, pu=pu, f=f, f4=f4, c=c, R=R: h.matmul(pu[:, f4 * 128:f4 * 128 + R], lhsT=wup[:, c, f * 128:(f + 1) * 128], rhs=x1T[:, c, 0:R], start=(c == 0), stop=(c == 7)),
                               r=["wup", "x1T"], w=[puk])
                    for f4 in range(4):
                        f = fg * 4 + f4
                        op("act", lambda h, pu=pu, rlt=rlt, f=f, f4=f4, R=R: h.activation(out=rlt[:, f4 * 128:f4 * 128 + R], in_=pu[:, f4 * 128:f4 * 128 + R], func=AF.Relu, bias=bup[:, f:f + 1]),
                           r=[puk, "bup"], w=[rlk])
                    op("pool", lambda h, rlt=rlt, fg=fg, R=R: h.tensor_tensor(out=hidT[:, fg * 4:(fg + 1) * 4, 0:R], in0=rlt[:, :].rearrange("p (a t) -> p a t", a=4)[:, :, 0:R],
                                                                        in1=rlt[:, :].rearrange("p (a t) -> p a t", a=4)[:, :, 0:R], op=ALU.mult), r=[rlk], w=["hidT"])
                for hf in range(2):
                    for f in range(32):
                        op("pe", lambda h, hf=hf, f=f, R=R: h.matmul(pM[hf][0:R, :], lhsT=hidT[:, f, 0:R], rhs=wdn[:, f, hf * 512:(hf + 1) * 512], start=(f == 0), stop=(f == 31)),
                           r=["hidT", "wdn"], w=["pM%d" % hf])
                for hf in range(2):
                    op("dve", lambda h, hf=hf, R=R: h.scalar_tensor_tensor(out=yt[0:R, hf * 512:(hf + 1) * 512], in0=x1[0:R, hf * 512:(hf + 1) * 512], scalar=ALPHA, in1=pM[hf][0:R, :],
                                                                      op0=ALU.mult, op1=ALU.add), r=["x1", "pM%d" % hf], w=["yt"])
                op("pool", lambda h, R=R: h.tensor_tensor(out=yt[0:R, :], in0=yt[0:R, :], in1=BD[0:R, :], op=ALU.add), r=["yt", "BD"], w=["yt"])
                layer_norm(yt, x1, G2, B2, "G2", "B2", R, "yt", "x1")
                dma("sp", lambda h, R=R, ydst=ydst: h.dma_start(out=ydst, in_=x1[0:R, :]), r=["x1"])
            sc_.flush()
    return nc


def core_inputs(inp, c, consts):
    f = lambda a: np.ascontiguousarray(np.asarray(a))
    npool = inp["cache_k"].shape[1]
    m = {
        "x": f(inp["x_prompt"][c]),
        "xs": f(inp["x_sample"][4 * c:4 * c + 4, 0, :]),
        "ck": f(inp["cache_k"][0]).reshape(npool * 128, 512),
        "cv": f(inp["cache_v"][0]).reshape(npool * 128, 512),
        "cki": f(inp["cache_k_idx"][0]).reshape(npool * 128, 64),
        "sh": f(inp["state_h"][0, 4 * c:4 * c + 4]),
        "scv": f(inp["state_conv"][0, 4 * c:4 * c + 4]),
        "pt": f(inp["page_table"][4 * c:4 * c + 4]).astype(np.int32),
        "relb": f(inp["rel_bias"]),
        "w_in": f(inp["w_in"][0]), "conv_w": f(inp["conv_w"][0]), "conv_b": f(inp["conv_b"][0]),
        "w_a": f(inp["w_a"][0]), "b_a": f(inp["b_a"][0]).reshape(-1), "w_x": f(inp["w_x"][0]), "b_x": f(inp["b_x"][0]).reshape(-1),
        "lam": f(inp["lru_lambda"][0]), "w_out": f(inp["w_out"][0]), "ln1_g": f(inp["ln1_g"][0]), "ln1_b": f(inp["ln1_b"][0]),
        "w_up": f(inp["w_up"][0]), "b_up": f(inp["b_up"][0]), "w_down": f(inp["w_down"][0]), "b_down": f(inp["b_down"][0]),
        "ln2_g": f(inp["ln2_g"][0]), "ln2_b": f(inp["ln2_b"][0]),
    }
    for k, v in consts.items():
        m["c_" + k] = v
    return m


def assemble(results, B, S, NSB):
    cat = lambda k: np.stack([np.asarray(r[k]) for r in results], axis=0)
    y_p = cat("y")
    y_s = cat("ys").reshape(NSB, 1, D)
    k_p = cat("kp").reshape(1, B, S, 8, 64)
    v_p = cat("vp").reshape(1, B, S, 8, 64)
    ki_p = cat("kip").reshape(1, B, S, 64)
    h_p = cat("hp").reshape(1, B, 512)
    c_p = cat("cp").reshape(1, B, 3, 512)
    k_s = cat("ksm").reshape(1, NSB, 1, 8, 64)
    v_s = cat("vsm").reshape(1, NSB, 1, 8, 64)
    ki_s = cat("kis").reshape(1, NSB, 1, 64)
    h_s = cat("hs").reshape(1, NSB, 512)
    c_s = cat("cs").reshape(1, NSB, 3, 512)
    return tuple(np.ascontiguousarray(a, dtype=np.float32) for a in (y_p, y_s, k_p, v_p, ki_p, h_p, c_p, k_s, v_s, ki_s, h_s, c_s))


def kernel(**inputs):
    B, S, _ = inputs["x_prompt"].shape
    NSB = inputs["x_sample"].shape[0]
    NPG = inputs["page_table"].shape[1]
    NPOOL = inputs["cache_k"].shape[1]
    ncores = B
    assert NSB == 4 * ncores
    nc = build(S, NPG, NPOOL)
    consts = make_consts(NPG)
    in_maps = [core_inputs(inputs, c, consts) for c in range(ncores)]
    res = run_bass_kernel_spmd(nc, in_maps, core_ids=list(range(ncores)))
    return assemble(res.results, B, S, NSB)
```

```python
import numpy as np
from contextlib import ExitStack
import concourse.bass as bass
import concourse.mybir as mybir
from concourse.bass_utils import run_bass_kernel_spmd

F32 = mybir.dt.float32
BF16 = mybir.dt.bfloat16
I32 = mybir.dt.int32
U8 = mybir.dt.uint8
AF = mybir.ActivationFunctionType
ALU = mybir.AluOpType
AX = mybir.AxisListType

D = 1024
DIN = 3144
DFF = 4096
ALPHA = 2.0 ** 0.25
EPS = 1e-5
NEG = -1.0e30
ENGS = ["pe", "act", "pool", "dve", "sp"]


class Sched:
    def __init__(self, nc, es, ndma=40):
        self.nc = nc
        self.lists = {e: [] for e in ENGS}
        self.seq = {e: 0 for e in ENGS}
        self.sem = {e: es.enter_context(nc.semaphore("sem_" + e)) for e in ENGS}
        self.ndma = ndma
        self.dsem = [es.enter_context(nc.semaphore("dsem%d" % i)) for i in range(ndma)]
        self.dval = [0] * ndma
        self.dlast = [None] * ndma
        self.dma_i = 0
        self.dma_ip = 0
        self.waited = {e: {} for e in ENGS}
        self.lastw = {}
        self.readers = {}

    def _semof(self, key):
        return self.sem[key[1]] if key[0] == "e" else self.dsem[key[1]]

    def _deps(self, eng, r, w, extra=()):
        needs = {}

        def need(tok, same_ok):
            if tok is None:
                return
            kind, ident, val = tok
            if kind == "e" and ident == eng and same_ok and eng == "pe":
                return
            k = (kind, ident)
            if needs.get(k, 0) < val:
                needs[k] = val

        for k in r:
            need(self.lastw.get(k), False)
        for k in w:
            need(self.lastw.get(k), True)
            for t in self.readers.get(k, {}).values():
                need(t, True)
        for t in extra:
            need(t, False)
        for k, val in needs.items():
            if self.waited[eng].get(k, 0) < val:
                self.waited[eng][k] = val
                sem = self._semof(k)
                self.lists[eng].append(lambda h, s=sem, v=val: h.wait_ge(s, v))

    def _mark(self, tok, r, w):
        for k in w:
            self.lastw[k] = tok
            self.readers[k] = {}
        for k in r:
            self.readers.setdefault(k, {})[(tok[0], tok[1])] = tok

    def op(self, eng, fn, r=(), w=()):
        self._deps(eng, r, w)
        self.seq[eng] += 1
        tok = ("e", eng, self.seq[eng])
        sem = self.sem[eng]
        self.lists[eng].append(lambda h, f=fn, s=sem: f(h).then_inc(s, 1))
        self._mark(tok, r, w)

    def dma(self, eng, fn, r=(), w=()):
        if eng == "pool":
            k = self.ndma - 8 + (self.dma_ip % 8)
            self.dma_ip += 1
        else:
            k = self.dma_i % (self.ndma - 8)
            self.dma_i += 1
        self._deps(eng, r, w, extra=(self.dlast[k],))
        self.dval[k] += 16
        tok = ("d", k, self.dval[k])
        self.dlast[k] = tok
        sem = self.dsem[k]
        self.lists[eng].append(lambda h, f=fn, s=sem: f(h).then_inc(s, 16))
        self._mark(tok, r, w)

    def flush(self, drain=True):
        nc = self.nc
        if drain:
            for k in range(self.ndma):
                t = self.dlast[k]
                if t is not None and self.waited["sp"].get(("d", k), 0) < t[2]:
                    self.waited["sp"][("d", k)] = t[2]
                    self.lists["sp"].append(lambda h, s=self.dsem[k], v=t[2]: h.wait_ge(s, v))
        lists = self.lists
        self.lists = {e: [] for e in ENGS}
        with nc.Block() as block:
            @block.tensor
            def _(h):
                for t in lists["pe"]:
                    t(h)

            @block.scalar
            def _(h):
                for t in lists["act"]:
                    t(h)

            @block.gpsimd
            def _(h):
                for t in lists["pool"]:
                    t(h)

            @block.vector
            def _(h):
                for t in lists["dve"]:
                    t(h)

            @block.sync
            def _(h):
                for t in lists["sp"]:
                    t(h)


def t5_bucket_np(dist):
    import math
    dist = np.maximum(dist, 0)
    d = np.maximum(dist, 1).astype(np.float32)
    large = 16 + (np.log(d / np.float32(16)) / np.float32(math.log(128 / 16)) * np.float32(16)).astype(np.int32)
    large = np.minimum(large, 31)
    return np.where(dist < 16, dist, large)


def make_consts(NPG=128):
    c = {}
    c["ident"] = np.eye(128, dtype=np.float32)
    q = np.arange(128)[:, None]
    s = np.arange(128)[None, :]
    c["tri"] = np.where(s <= q, 0.0, NEG).astype(np.float32)
    b = t5_bucket_np(np.arange(256))
    oh = np.zeros((32, 256), np.float32)
    oh[b, np.arange(256)] = 1.0
    oh[31, :] -= 1.0
    c["ohb"] = oh
    thr = np.array([int(np.argmax(b >= j)) for j in range(1, 32)], np.float32)
    c["bthr"] = np.tile(thr[None, :], (128, 1)).astype(np.float32)
    c["iota32"] = np.tile(np.arange(32, dtype=np.float32)[None, :], (128, 1))
    c["iota256"] = np.tile(np.arange(256, dtype=np.float32)[None, :], (128, 1))
    c["iotap"] = np.arange(128, dtype=np.float32)[:, None].copy()
    c["ones"] = np.ones((128, 128), np.float32)
    c["anti"] = np.eye(128, dtype=np.float32)[::-1].copy()
    su = np.triu(np.ones((128, 128), np.float32), 1)
    c["sut"] = su
    selfm = np.full((128, 1), NEG, np.float32)
    selfm[0, 0] = 0.0
    c["selfm"] = selfm
    p = np.arange(128, dtype=np.float32)[:, None]
    o = np.arange(129, dtype=np.float32)[None, :]
    pos = p * 128 + o
    pos[:, 128] = NPG * 128
    c["posc"] = pos.astype(np.float32)
    off = np.tile(o, (128, 1)).astype(np.float32)
    off[:, 128] = 0
    c["offc"] = off
    return c


CONST_SHAPES = {"ident": [128, 128], "tri": [128, 128], "ohb": [32, 256], "bthr": [128, 31],
                "anti": [128, 128], "iota32": [128, 32], "iota256": [128, 256], "iotap": [128, 1], "ones": [128, 128],
                "sut": [128, 128], "selfm": [128, 1], "posc": [128, 129], "offc": [128, 129]}


def build(S, NPG, NPOOL, with_sample=True, with_passB=True, NIT=19, stage=9):
    NT = S // 128
    TOPK = min(256, S // 4)
    nc = bass.Bass("TRN2", target_bir_lowering=False)

    def din(name, shape, dt=F32):
        return nc.dram_tensor(name, list(shape), dt, kind="ExternalInput").ap()

    def dout(name, shape, dt=F32):
        return nc.dram_tensor(name, list(shape), dt, kind="ExternalOutput").ap()

    x = din("x", [S, D]); xs = din("xs", [4, D])
    ck = din("ck", [NPOOL * 128, 512]); cv = din("cv", [NPOOL * 128, 512]); cki = din("cki", [NPOOL * 128, 64])
    sh = din("sh", [4, 512]); scv = din("scv", [4, 3, 512]); pt = din("pt", [4, NPG], I32)
    relb = din("relb", [32, 8]); w_in = din("w_in", [D, DIN]); conv_w = din("conv_w", [4, 512]); conv_b = din("conv_b", [512])
    w_a = din("w_a", [8, 64, 64]); b_a = din("b_a", [512]); w_x = din("w_x", [8, 64, 64]); b_x = din("b_x", [512])
    lam = din("lam", [512]); w_out = din("w_out", [D, D]); ln1_g = din("ln1_g", [D]); ln1_b = din("ln1_b", [D])
    w_up = din("w_up", [D, DFF]); b_up = din("b_up", [DFF]); w_down = din("w_down", [DFF, D]); b_down = din("b_down", [D])
    ln2_g = din("ln2_g", [D]); ln2_b = din("ln2_b", [D])
    C = {k: din("c_" + k, v) for k, v in CONST_SHAPES.items()}

    y = dout("y", [S, D]); ys = dout("ys", [4, D]); kp = dout("kp", [S, 512]); vp = dout("vp", [S, 512]); kip = dout("kip", [S, 64])
    hp = dout("hp", [512]); cp = dout("cp", [3, 512]); ksm = dout("ksm", [4, 512]); vsm = dout("vsm", [4, 512]); kis = dout("kis", [4, 64])
    hs = dout("hs", [4, 512]); cs = dout("cs", [4, 3, 512])
    catd = nc.dram_tensor("catd", [S + 128, D], BF16, kind="Internal").ap()
    tsc = nc.dram_tensor("tsc", [8, 512], F32, kind="Internal").ap()
    sst = nc.dram_tensor("sst", [4, 2120], F32, kind="Internal").ap()
    PAST = NPG * 128
    TOPK_S = min(256, (PAST + 1) // 4)

    with ExitStack() as es0:
        sc_ = Sched(nc, es0)
        op, dma = sc_.op, sc_.dma
        es0.enter_context(nc.allow_non_contiguous_dma(reason="small strided param loads"))

        def sb0(n, s, d=F32):
            return es0.enter_context(nc.sbuf_tensor(n, s, d))
        identb = sb0("identb", [128, 128], BF16)
        dma("pool", lambda h: h.dma_start(out=identb[:], in_=C["ident"]), w=["identb"])

        with ExitStack() as es:
            def sb(n, s, d=F32):
                return es.enter_context(nc.sbuf_tensor(n, s, d))

            def ps(n, s, d=F32):
                return es.enter_context(nc.psum_tensor(n, s, d))
            win = sb("win", [128, 8, DIN], BF16)
            KT = sb("KT", [128, 4, S], BF16)
            V = sb("V", [128, NT, 8, 65], BF16)
            KIT = sb("KIT", [128, S], BF16)
            MT = [sb("MT%d" % i, [128, NT, 128], BF16) for i in range(2)]
            scr = sb("scr", [128, S], F32)
            junk = sb("junk", [128, max(S, 4096)], U8)
            Rbuf = [junk[:, 1024 * b4:1024 * (b4 + 1)].bitcast(BF16) for b4 in range(4)]
            xin = sb("xin", [128, D], F32)
            xb = sb("xb", [128, D], BF16)
            Dg = xb
            xT = sb("xT", [128, 8, 128], BF16)
            QT = [sb("QT%d" % i, [128, 4, 128], BF16) for i in range(3)]
            QIT = sb("QIT", [128, 4, 128], BF16)
            kst = sb("kst", [128, 512]); vst = sb("vst", [128, 512]); kist = sb("kist", [128, 72])
            wiT = sb("wiT", [128, 8]); absw = sb("absw", [128, 8]); sgn = sb("sgn", [128, 8])
            E = [sb("E%d" % i, [128, 512], BF16) for i in range(2)]
            Pm = [sb("Pm%d" % i, [128, 512], BF16) for i in range(2)]
            mk = sb("mk", [128, 512], BF16)
            catA = sb("catA", [128, 512], BF16); catR = sb("catR", [128, 512], BF16)
            rnnT = sb("rnnT", [128, 4, 128], BF16)
            XR = sb("XR", [128, 4, 131]); GR = sb("GR", [128, 4, 128]); xc = sb("xc", [128, 4, 128])
            rr = sb("rr", [128, 4, 128]); ii = sb("ii", [128, 4, 128]); uu = sb("uu", [128, 4, 128])
            HH = sb("HH", [128, 4, 128]); xrc = sb("xrc", [128, 4, 3])
            EB = sb("EB", [128, 8, 2, 128], BF16)
            EBf = kst[:, 0:128]
            tri = sb("tri", [128, 128])
            WA = sb("WA", [128, 4, 128]); WX = sb("WX", [128, 4, 128])
            cw = sb("cw", [128, 4, 4]); cb = sb("cb", [128, 4]); ba = sb("ba", [128, 4]); bx = sb("bx", [128, 4])
            c8 = sb("c8", [128, 4]); hst = sb("hst", [128, 4])
            lo = sb("lo", [128, 1]); mid = sb("mid", [128, 1]); cnt = sb("cnt", [128, 1]); gg = sb("gg", [128, 1])
            rec = sb("rec", [128, 8])
            rb = scr[0:32, 0:8]; ohb = scr[0:32, 8:264]; Tt = vst[0:8, :]

            pA = ps("pA", [128, 512]); pB = ps("pB", [128, 512]); pT = ps("pT", [128, 1024], BF16)
            pI = [ps("pI0", [128, 512]), pB]
            pIk = ["pI0", "pB"]
            pS = [ps("pS%d" % i, [128, 512]) for i in range(2)]
            pOf = [ps("pO%d" % i, [128, 512]) for i in range(2)]
            pOt = [t[:, 0:260].rearrange("p (a d) -> p a d", a=4) for t in pOf]

            def pOv(hh):
                return pOt[hh // 4][:, hh % 4, :]

            def pOk(hh):
                return "pO%d" % (hh // 4)

            w_in_v = w_in.rearrange("(c p) n -> p c n", p=128)
            for c in range(8):
                for (a, b) in ((0, 1572), (1572, DIN)):
                    dma("pool", lambda h, c=c, a=a, b=b: h.dma_start(out=win[:, c, a:b], in_=w_in_v[:, c, a:b]), w=["win"])
            dma("sp", lambda h: h.dma_start(out=tri[:], in_=C["tri"]), w=["tri"])
            anti = sb("anti", [128, 128])
            dma("sp", lambda h: h.dma_start(out=anti[:], in_=C["anti"]), w=["anti"])
            op("pool", lambda h: h.memset(WA[:], 0.0), w=["WA"])
            op("pool", lambda h: h.memset(WX[:], 0.0), w=["WX"])
            for n in range(8):
                k, nl = n // 2, n % 2
                dma("sp", lambda h, n=n, k=k, nl=nl: h.dma_start(out=WA[nl * 64:(nl + 1) * 64, k, nl * 64:(nl + 1) * 64], in_=w_a[n]), w=["WA"])
                dma("sp", lambda h, n=n, k=k, nl=nl: h.dma_start(out=WX[nl * 64:(nl + 1) * 64, k, nl * 64:(nl + 1) * 64], in_=w_x[n]), w=["WX"])
            for j in range(4):
                dma("sp", lambda h, j=j: h.dma_start(out=cw[:, :, j], in_=conv_w[j].rearrange("(k p) -> p k", p=128)), w=["cw"])
            for nm, tl, src in (("cb", cb, conv_b), ("ba", ba, b_a), ("bx", bx, b_x), ("c8", c8, lam)):
                dma("sp", lambda h, tl=tl, src=src: h.dma_start(out=tl[:], in_=src.rearrange("(k p) -> p k", p=128)), w=[nm])
            op("act", lambda h: h.activation(out=c8[:], in_=c8[:], func=AF.Exp, scale=-1.0), r=["c8"], w=["c8"])
            op("act", lambda h: h.activation(out=c8[:], in_=c8[:], func=AF.Ln, bias=1.0), r=["c8"], w=["c8"])
            op("dve", lambda h: h.tensor_scalar(out=c8[:], in0=c8[:], scalar1=-8.0, scalar2=None, op0=ALU.mult), r=["c8"], w=["c8"])
            dma("sp", lambda h: h.dma_start(out=rb, in_=relb), w=[("scr", 0)])
            dma("sp", lambda h: h.dma_start(out=ohb, in_=C["ohb"]), w=[("scr", 0)])
            op("pe", lambda h: h.matmul(pA[0:8, 0:256], lhsT=rb, rhs=ohb, start=True, stop=True), r=[("scr", 0)], w=["pA"])
            op("dve", lambda h: h.memset(Tt, -30000.0), w=["vst"])
            op("dve", lambda h: h.tensor_copy(out=vst[0:8, 128:384], in_=pA[0:8, 0:256]), r=["pA"], w=["vst"])
            dma("sp", lambda h: h.dma_start(out=tsc, in_=Tt), r=["vst"], w=["tsc"])
            for hh in range(8):
                for dl in range(2):
                    src = bass.AP(tensor=tsc.tensor, offset=hh * 512 + 1 + 128 * dl, ap=[[1, 128], [1, 128]])
                    dma("sp", lambda h, src=src: h.dma_start(out=EBf, in_=src), r=["tsc"], w=["kst"])
                    op("pe", lambda h: h.matmul(pA[:, 0:128], lhsT=anti[:], rhs=EBf, start=True, stop=True), r=["anti", "kst"], w=["pA"])
                    op("act", lambda h, hh=hh, dl=dl: h.activation(out=EB[:, hh, dl, :], in_=pA[:, 0:128], func=AF.Exp), r=["pA"], w=["EB"])
            op("pool", lambda h: h.memset(V[:, :, :, 64:65], 1.0), w=["Vones"])
            op("pool", lambda h: h.memset(XR[:, :, 0:3], 0.0), w=["XR"])

            def load_x(i):
                dma("sp", lambda h, i=i: h.dma_start(out=xin[:], in_=x[i * 128:(i + 1) * 128, :]), w=["xin"])

            if stage >= 1:
                load_x(0)

            def fm_group(pbank, pkey, col0, nchunk, dup=False):
                if dup:
                    for hf in range(2):
                        for c in range(8):
                            op("pe", lambda h, hf=hf, c=c: h.matmul(pbank[hf * 64:(hf + 1) * 64, 0:128], lhsT=win[:, c, col0:col0 + 64], rhs=xT[:, c, :], start=(c == 0), stop=(c == 7)),
                               r=["win", "xT"], w=[pkey])
                    return
                for k in range(nchunk):
                    for c in range(8):
                        l = win[:, c, col0 + k * 128: col0 + (k + 1) * 128]
                        op("pe", lambda h, l=l, k=k, c=c: h.matmul(pbank[:, k * 128:(k + 1) * 128], lhsT=l, rhs=xT[:, c, :], start=(c == 0), stop=(c == 7)),
                           r=["win", "xT"], w=[pkey])

            def tm_group(pbank, pkey, col0, n):
                for c in range(8):
                    op("pe", lambda h, c=c: h.matmul(pbank[:, 0:n], lhsT=xT[:, c, :], rhs=win[:, c, col0:col0 + n], start=(c == 0), stop=(c == 7)),
                       r=["win", "xT"], w=[pkey])

            def stage_A(i):
                QTi = QT[i % 3]; qk = "QT%d" % (i % 3)
                op("act", lambda h: h.activation(out=xb[:], in_=xin[:], func=AF.Identity), r=["xin"], w=["xb"])
                if i + 1 < NT:
                    load_x(i + 1)
                for c in range(8):
                    op("pe", lambda h, c=c: h.transpose(pT[:, c * 128:(c + 1) * 128], xb[:, c * 128:(c + 1) * 128], identb[:]), r=["xb", "identb"], w=["pT"])
                op("act", lambda h: h.activation(out=xT[:].rearrange("p c t -> p (c t)"), in_=pT[:, :], func=AF.Identity), r=["pT"], w=["xT"])
                fm_group(pA, "pA", 0, 4)
                op("act", lambda h: h.activation(out=QTi[:].rearrange("p k t -> p (k t)"), in_=pA[:, :], func=AF.Identity, scale=0.125), r=["pA"], w=[qk])
                fm_group(pB, "pB", 512, 4)
                op("act", lambda h: h.activation(out=KT[:, :, i * 128:(i + 1) * 128], in_=pB[:, :].rearrange("p (k t) -> p k t", k=4), func=AF.Identity), r=["pB"], w=[("KT", i)])
                fm_group(pA, "pA", 1536, 4)
                op("act", lambda h: h.activation(out=QIT[:].rearrange("p k t -> p (k t)"), in_=pA[:, :], func=AF.Identity), r=["pA"], w=["QIT"])
                fm_group(pB, "pB", 2048, 1, dup=True)
                op("act", lambda h: h.activation(out=KIT[:, i * 128:(i + 1) * 128], in_=pB[:, 0:128], func=AF.Identity), r=["pB"], w=[("KIT", i)])
                fm_group(pA, "pA", 2120, 4)
                op("act", lambda h: h.activation(out=XR[:, :, 3:131], in_=pA[:, :].rearrange("p (k t) -> p k t", k=4), func=AF.Identity), r=["pA"], w=["XR"])
                fm_group(pB, "pB", 2632, 4)
                op("act", lambda h: h.activation(out=GR[:].rearrange("p k t -> p (k t)"), in_=pB[:, :], func=AF.Identity), r=["pB"], w=["GR"])
                tm_group(pA, "pA", 512, 512)
                op("act", lambda h: h.activation(out=kst[:], in_=pA[:, :], func=AF.Identity), r=["pA"], w=["kst"])
                dma("sp", lambda h: h.dma_start(out=kp[i * 128:(i + 1) * 128, :], in_=kst[:]), r=["kst"])
                tm_group(pB, "pB", 1024, 512)
                op("act", lambda h: h.activation(out=vst[:], in_=pB[:, :], func=AF.Identity), r=["pB"], w=["vst"])
                dma("sp", lambda h: h.dma_start(out=vp[i * 128:(i + 1) * 128, :], in_=vst[:]), r=["vst"])
                op("pool", lambda h: h.tensor_copy(out=V[:, i, :, 0:64], in_=vst[:].rearrange("p (a d) -> p a d", a=8)), r=["vst"], w=[("V", i)])
                tm_group(pA, "pA", 2048, 72)
                op("act", lambda h: h.activation(out=kist[:], in_=pA[:, 0:72], func=AF.Identity), r=["pA"], w=["kist"])
                dma("sp", lambda h: h.dma_start(out=kip[i * 128:(i + 1) * 128, :], in_=kist[:, 0:64]), r=["kist"])
                op("act", lambda h: h.activation(out=wiT[:], in_=kist[:, 64:72], func=AF.Identity), r=["kist"], w=["wiT"])

            def stage_I(i):
                L = 128 * (i + 1)
                nkc = (L + 511) // 512
                op("dve", lambda h: h.tensor_scalar(out=sgn[:], in0=wiT[:], scalar1=0.0, scalar2=2.0, op0=ALU.is_gt, op1=ALU.mult), r=["wiT"], w=["sgn"])
                op("dve", lambda h: h.tensor_scalar(out=sgn[:], in0=sgn[:], scalar1=-1.0, scalar2=None, op0=ALU.add), r=["sgn"], w=["sgn"])
                op("dve", lambda h: h.tensor_tensor(out=absw[:], in0=wiT[:], in1=sgn[:], op=ALU.mult), r=["wiT", "sgn"], w=["absw"])
                for hh in range(8):
                    op("dve", lambda h, hh=hh: h.tensor_scalar(out=Dg[:, hh * 128:(hh + 1) * 128], in0=identb[:], scalar1=sgn[:, hh:hh + 1], scalar2=None, op0=ALU.mult),
                       r=["identb", "sgn"], w=["xb"])
                units = [(kc, hh) for kc in range(nkc) for hh in range(8)]

                def emit_dots(ui):
                    kc, hh = units[ui]
                    wk = min(512, L - 512 * kc)
                    hq, pr = hh % 2, hh // 2
                    pb = pI[ui % 2]; pk = pIk[ui % 2]
                    ktl = [("KIT", j) for j in range(4 * kc, 4 * kc + wk // 128)]
                    op("pe", lambda h: h.matmul(pb[:, 0:wk], lhsT=QIT[hq * 64:(hq + 1) * 64, pr, :], rhs=KIT[hq * 64:(hq + 1) * 64, kc * 512:kc * 512 + wk], start=True, stop=True),
                       r=["QIT"] + ktl, w=[pk])

                emit_dots(0)
                for ui, (kc, hh) in enumerate(units):
                    wk = min(512, L - 512 * kc)
                    pb = pI[ui % 2]; pk = pIk[ui % 2]
                    Rb_ = Rbuf[ui % 4]; rk = ("junkR", ui % 4)
                    op("act", lambda h, pb=pb, Rb_=Rb_, wk=wk, hh=hh: h.activation(out=Rb_[:, 0:wk], in_=pb[:, 0:wk], func=AF.Relu, scale=absw[:, hh:hh + 1]), r=[pk, "absw", "junk"], w=[rk])
                    if ui + 1 < len(units):
                        emit_dots(ui + 1)
                    op("pe", lambda h, Rb_=Rb_, wk=wk, hh=hh: h.matmul(pA[:, 0:wk], lhsT=Dg[:, hh * 128:(hh + 1) * 128], rhs=Rb_[:, 0:wk], start=(hh == 0), stop=(hh == 7)),
                       r=["xb", rk], w=["pA"])
                    if hh == 7:
                        op("dve", lambda h, wk=wk, kc=kc: h.tensor_copy(out=scr[:, kc * 512:kc * 512 + wk], in_=pA[:, 0:wk]), r=["pA"], w=[("scr", kc)])
                kcd = i // 4
                op("dve", lambda h: h.tensor_tensor(out=scr[:, i * 128:(i + 1) * 128], in0=scr[:, i * 128:(i + 1) * 128], in1=tri[:], op=ALU.add), r=[("scr", kcd), "tri"], w=[("scr", kcd)])

            def stage_T(i):
                L = 128 * (i + 1)
                nkc = (L + 511) // 512
                sck = [("scr", kc) for kc in range(nkc)]
                jk = ["junk"] + [("junkR", b4) for b4 in range(4)]
                if L > TOPK:
                    op("dve", lambda h: h.memset(mid[:], 0.0), w=["mid"])
                    for it in range(NIT):
                        hk = 256.0 / (2.0 ** it)
                        op("dve", lambda h: h.tensor_scalar(out=junk[:, 0:L], in0=scr[:, 0:L], scalar1=mid[:, 0:1], scalar2=None, op0=ALU.is_gt, op1=ALU.add, accum_out=cnt[:, 0:1]),
                           r=sck + ["mid"], w=jk + ["cnt"])
                        op("dve", lambda h, hk=hk: h.tensor_scalar(out=gg[:], in0=cnt[:], scalar1=float(TOPK), scalar2=hk, op0=ALU.is_ge, op1=ALU.mult), r=["cnt"], w=["gg"])
                        op("dve", lambda h, hk=hk: h.scalar_tensor_tensor(out=mid[:], in0=mid[:], scalar=-hk / 2.0, in1=gg[:], op0=ALU.add, op1=ALU.add), r=["mid", "gg"], w=["mid"])
                    hN = 256.0 / (2.0 ** NIT)
                    op("dve", lambda h: h.tensor_scalar(out=lo[:], in0=mid[:], scalar1=-hN, scalar2=None, op0=ALU.add), r=["mid"], w=["lo"])
                else:
                    op("dve", lambda h: h.memset(lo[:], -1.0e29), w=["lo"])

            def stage_M(i):
                L = 128 * (i + 1)
                nkc = (L + 511) // 512
                MTi = MT[i % 2]
                for kc in range(nkc):
                    wk = min(512, L - 512 * kc)
                    nb = wk // 128
                    op("dve", lambda h, kc=kc, wk=wk: h.tensor_scalar(out=mk[:, 0:wk], in0=scr[:, kc * 512:kc * 512 + wk], scalar1=lo[:, 0:1], scalar2=None, op0=ALU.is_gt),
                       r=[("scr", kc), "lo"], w=["mk"])
                    for jj in range(nb):
                        op("pe", lambda h, jj=jj: h.transpose(pT[:, jj * 128:(jj + 1) * 128], mk[:, jj * 128:(jj + 1) * 128], identb[:]), r=["mk", "identb"], w=["pT"])
                    op("act", lambda h, kc=kc, nb=nb, wk=wk: h.activation(out=MTi[:, 4 * kc:4 * kc + nb, :].rearrange("p j q -> p (j q)"), in_=pT[:, 0:wk], func=AF.Identity),
                       r=["pT"], w=[("MT", i % 2, kc)])

            def stage_AT(i):
                QTi = QT[i % 3]; qk = "QT%d" % (i % 3); MTi = MT[i % 2]
                units = [(hh, g) for hh in range(8) for g in range((i + 4) // 4)]

                def emit_qk(u, b2):
                    hh, g = u
                    hq, pr = hh % 2, hh // 2
                    j0 = 4 * g
                    n = min(j0 + 3, i) - j0 + 1
                    psb = pS[b2]
                    for jj in range(n):
                        j = j0 + jj
                        op("pe", lambda h, jj=jj, j=j: h.matmul(psb[:, jj * 128:(jj + 1) * 128], lhsT=KT[hq * 64:(hq + 1) * 64, pr, j * 128:(j + 1) * 128],
                                                              rhs=QTi[hq * 64:(hq + 1) * 64, pr, :], start=True, stop=True),
                           r=[("KT", j), qk], w=["pS%d" % b2])

                if units:
                    emit_qk(units[0], 0)
                for ui, (hh, g) in enumerate(units):
                    b2 = ui % 2
                    j0 = 4 * g
                    n = min(j0 + 3, i) - j0 + 1
                    psb = pS[b2]; Eb = E[b2]; Pb = Pm[b2]
                    op("act", lambda h, psb=psb, Eb=Eb, n=n: h.activation(out=Eb[:, 0:n * 128], in_=psb[:, 0:n * 128], func=AF.Exp), r=["pS%d" % b2], w=["E%d" % b2])
                    op("pool", lambda h, Eb=Eb, Pb=Pb, n=n, j0=j0: h.tensor_tensor(out=Pb[:, 0:n * 128], in0=Eb[:, 0:n * 128],
                                                                               in1=MTi[:, j0:j0 + n, :].rearrange("p j q -> p (j q)"), op=ALU.mult),
                       r=["E%d" % b2, ("MT", i % 2, g)], w=["Pm%d" % b2])
                    for jj in range(n):
                        j = j0 + jj
                        if i - j <= 1:
                            dl = i - j
                            op("pool", lambda h, Pb=Pb, jj=jj, hh=hh, dl=dl: h.tensor_tensor(out=Pb[:, jj * 128:(jj + 1) * 128], in0=Pb[:, jj * 128:(jj + 1) * 128],
                                                                                         in1=EB[:, hh, dl, :], op=ALU.mult),
                               r=["Pm%d" % b2, "EB"], w=["Pm%d" % b2])
                    if ui + 1 < len(units):
                        emit_qk(units[ui + 1], (ui + 1) % 2)
                    for jj in range(n):
                        j = j0 + jj
                        op("pe", lambda h, Pb=Pb, jj=jj, j=j, hh=hh, st=(j == 0), sp=(j == i): h.matmul(pOv(hh), lhsT=Pb[:, jj * 128:(jj + 1) * 128], rhs=V[:, j, hh, :], start=st, stop=sp),
                           r=["Pm%d" % b2, ("V", j), "Vones"], w=[pOk(hh)])
                for half in range(2):
                    pv = pOt[half]
                    op("dve", lambda h, pv=pv, half=half: h.reciprocal(out=rec[:, half * 4:(half + 1) * 4], in_=pv[:, :, 64]), r=["pO%d" % half], w=["rec"])
                    for a4 in range(4):
                        hh = half * 4 + a4
                        op("act", lambda h, pv=pv, a4=a4, hh=hh: h.activation(out=catA[:, hh * 64:(hh + 1) * 64], in_=pv[:, a4, 0:64], func=AF.Identity, scale=rec[:, hh:hh + 1]),
                           r=["pO%d" % half, "rec"], w=["catA"])
                dma("sp", lambda h: h.dma_start(out=catd[i * 128:(i + 1) * 128, 0:512], in_=catA[:]), r=["catA"], w=[("catd", i, 0)])

            UUK = [("uu", k) for k in range(4)]; XCK = [("xc", k) for k in range(4)]

            def stage_R1(i):
                for k in range(4):
                    op("dve", lambda h, k=k: h.tensor_scalar(out=xc[:, k, :], in0=XR[:, k, 3:131], scalar1=cw[:, k, 3:4], scalar2=cb[:, k:k + 1], op0=ALU.mult, op1=ALU.add),
                       r=["XR", "cw", "cb"], w=[("xc", k)])
                    for j in range(3):
                        op("dve", lambda h, k=k, j=j: h.scalar_tensor_tensor(out=xc[:, k, :], in0=XR[:, k, j:j + 128], scalar=cw[:, k, j:j + 1], in1=xc[:, k, :], op0=ALU.mult, op1=ALU.add),
                           r=["XR", "cw", ("xc", k)], w=[("xc", k)])
                op("pool", lambda h: h.tensor_copy(out=xrc[:], in_=XR[:, :, 128:131]), r=["XR"], w=["xrc"])
                if i == NT - 1:
                    for j in range(3):
                        dma("sp", lambda h, j=j: h.dma_start(out=cp[j].rearrange("(k p) -> p k", p=128), in_=xrc[:, :, j]), r=["xrc"])
                op("pool", lambda h: h.tensor_copy(out=XR[:, :, 0:3], in_=xrc[:]), r=["xrc"], w=["XR"])
                for k in range(4):
                    op("pe", lambda h, k=k: h.matmul(pA[:, k * 128:(k + 1) * 128], lhsT=WA[:, k, :], rhs=xc[:, k, :], start=True, stop=True), r=["WA", ("xc", k)], w=["pA"])
                    op("pe", lambda h, k=k: h.matmul(pB[:, k * 128:(k + 1) * 128], lhsT=WX[:, k, :], rhs=xc[:, k, :], start=True, stop=True), r=["WX", ("xc", k)], w=["pB"])
                for k in range(4):
                    op("act", lambda h, k=k: h.activation(out=rr[:, k, :], in_=pA[:, k * 128:(k + 1) * 128], func=AF.Sigmoid, bias=ba[:, k:k + 1]), r=["pA", "ba"], w=["rr"])
                    op("act", lambda h, k=k: h.activation(out=ii[:, k, :], in_=pB[:, k * 128:(k + 1) * 128], func=AF.Sigmoid, bias=bx[:, k:k + 1]), r=["pB", "bx"], w=["ii"])
                for k in range(4):
                    op("act", lambda h, k=k: h.activation(out=rr[:, k, :], in_=rr[:, k, :], func=AF.Exp, scale=c8[:, k:k + 1]), r=["rr", "c8"], w=["rr"])
                op("pool", lambda h: h.tensor_tensor(out=uu[:], in0=rr[:], in1=rr[:], op=ALU.mult), r=["rr"], w=UUK)
                op("act", lambda h: h.activation(out=uu[:], in_=uu[:], func=AF.Sqrt, scale=-1.0, bias=1.0), r=UUK, w=UUK)
                op("pool", lambda h: h.tensor_tensor(out=uu[:], in0=uu[:], in1=ii[:], op=ALU.mult), r=UUK + ["ii"], w=UUK)
                op("pool", lambda h: h.tensor_tensor(out=uu[:], in0=uu[:], in1=xc[:], op=ALU.mult), r=UUK + XCK, w=UUK)
                op("pool", lambda h: h.tensor_tensor(out=ii[:], in0=GR[:], in1=GR[:], op=ALU.mult), r=["GR"] + UUK, w=["ii"])
                op("pool", lambda h: h.tensor_scalar(out=ii[:], in0=ii[:], scalar1=0.044715, scalar2=1.0, op0=ALU.mult, op1=ALU.add), r=["ii"], w=["ii"])
                op("pool", lambda h: h.tensor_tensor(out=ii[:], in0=ii[:], in1=GR[:], op=ALU.mult), r=["ii", "GR"], w=["ii"])
                op("act", lambda h: h.activation(out=ii[:], in_=ii[:], func=AF.Sigmoid, scale=1.5957691216057308), r=["ii"], w=["ii"])
                op("pool", lambda h: h.tensor_tensor(out=ii[:], in0=ii[:], in1=GR[:], op=ALU.mult), r=["ii", "GR"], w=["ii"])

            def stage_R2(i):
                for k in range(4):
                    init = 0.0 if i == 0 else hst[:, k:k + 1]
                    op("dve", lambda h, k=k, init=init: h.tensor_tensor_scan(out=HH[:, k, :], data0=rr[:, k, :], data1=uu[:, k, :], initial=init, op0=ALU.mult, op1=ALU.add),
                       r=["rr", ("uu", k), "hst"], w=["HH"])
                op("dve", lambda h: h.tensor_copy(out=hst[:], in_=HH[:, :, 127]), r=["HH"], w=["hst"])
                if i == NT - 1:
                    dma("sp", lambda h: h.dma_start(out=hp.rearrange("(k p) -> p k", p=128), in_=hst[:]), r=["hst"])
                op("pool", lambda h: h.tensor_tensor(out=rnnT[:], in0=ii[:], in1=HH[:], op=ALU.mult), r=["ii", "HH"], w=["rnnT"])
                for k in range(4):
                    op("pe", lambda h, k=k: h.transpose(pT[:, k * 128:(k + 1) * 128], rnnT[:, k, :], identb[:]), r=["rnnT", "identb"], w=["pT"])
                op("act", lambda h: h.activation(out=catR[:], in_=pT[:, 0:512], func=AF.Identity), r=["pT"], w=["catR"])
                dma("sp", lambda h: h.dma_start(out=catd[i * 128:(i + 1) * 128, 512:1024], in_=catR[:]), r=["catR"], w=[("catd", i, 1)])

            if stage >= 1:
                stage_A(0)
                stage_I(0)
                stage_R1(0)
                for i in range(NT):
                    if i + 1 < NT:
                        stage_A(i + 1)
                    stage_T(i)
                    if i > 0:
                        stage_AT(i - 1)
                    stage_M(i)
                    stage_R2(i)
                    if i + 1 < NT:
                        stage_I(i + 1)
                        stage_R1(i + 1)
                stage_AT(NT - 1)

            if with_sample:
                scS0 = sb("scS0", [128, 4, 12]); h0 = sb("h0", [128, 4, 4])
                dma("sp", lambda h: h.dma_start(out=xin[0:4, :], in_=xs), w=["xin"])
                for b in range(4):
                    for j in range(3):
                        dma("sp", lambda h, b=b, j=j: h.dma_start(out=scS0[:, :, j * 4 + b], in_=scv[b, j].rearrange("(k p) -> p k", p=128)), w=["scS0"])
                    dma("sp", lambda h, b=b: h.dma_start(out=h0[:, :, b], in_=sh[b].rearrange("(k p) -> p k", p=128)), w=["h0"])
                op("act", lambda h: h.activation(out=xb[0:4, :], in_=xin[0:4, :], func=AF.Identity), r=["xin"], w=["xb"])
                for c in range(8):
                    op("pe", lambda h, c=c: h.transpose(pT[:, c * 128:c * 128 + 4], xb[0:4, c * 128:(c + 1) * 128], identb[0:4, 0:4]), r=["xb", "identb"], w=["pT"])
                op("dve", lambda h: h.tensor_copy(out=xT[:, :, 0:4], in_=pT[:, :].rearrange("p (c t) -> p c t", c=8)[:, :, 0:4]), r=["pT"], w=["xT"])
                grp = [(0, 512, kst, "kst", None), (512, 512, vst, "vst", ksm), (1024, 512, kst, "kst", vsm), (1536, 512, vst, "vst", None), (2048, 72, kist, "kist", kis)]
                for gi, (c0, n, stg, sk, outd) in enumerate(grp):
                    pb_, pk_ = (pA, "pA") if gi % 2 == 0 else (pB, "pB")
                    for c in range(8):
                        op("pe", lambda h, c=c, pb_=pb_, c0=c0, n=n: h.matmul(pb_[0:4, 0:n], lhsT=xT[:, c, 0:4], rhs=win[:, c, c0:c0 + n], start=(c == 0), stop=(c == 7)), r=["win", "xT"], w=[pk_])
                    op("act", lambda h, pb_=pb_, stg=stg, n=n: h.activation(out=stg[0:4, 0:n], in_=pb_[0:4, 0:n], func=AF.Identity), r=[pk_], w=[sk])
                    dma("sp", lambda h, stg=stg, c0=c0, n=n: h.dma_start(out=sst[:, c0:c0 + n], in_=stg[0:4, 0:n]), r=[sk], w=["sst"])
                    if outd is not None:
                        dma("sp", lambda h, stg=stg, outd=outd: h.dma_start(out=outd, in_=stg[0:4, 0:outd.shape[1]]), r=[sk])
                for (c0, pb_, pk_) in ((2120, pA, "pA"), (2632, pB, "pB")):
                    for k in range(4):
                        for c in range(8):
                            op("pe", lambda h, k=k, c=c, pb_=pb_, c0=c0: h.matmul(pb_[:, k * 128:k * 128 + 4], lhsT=win[:, c, c0 + k * 128:c0 + (k + 1) * 128], rhs=xT[:, c, 0:4], start=(c == 0), stop=(c == 7)),
                               r=["win", "xT"], w=[pk_])
                op("act", lambda h: h.activation(out=XR[:, :, 3:7], in_=pA[:, :].rearrange("p (k t) -> p k t", k=4)[:, :, 0:4], func=AF.Identity), r=["pA"], w=["XR"])
                op("dve", lambda h: h.tensor_copy(out=GR[:, :, 0:4], in_=pB[:, :].rearrange("p (k t) -> p k t", k=4)[:, :, 0:4]), r=["pB"], w=["GR"])
                for k in range(4):
                    op("dve", lambda h, k=k: h.tensor_scalar(out=xc[:, k, 0:4], in0=XR[:, k, 3:7], scalar1=cw[:, k, 3:4], scalar2=cb[:, k:k + 1], op0=ALU.mult, op1=ALU.add), r=["XR", "cw", "cb"], w=["xc"])
                    for j in range(3):
                        op("dve", lambda h, k=k, j=j: h.scalar_tensor_tensor(out=xc[:, k, 0:4], in0=scS0[:, k, j * 4:(j + 1) * 4], scalar=cw[:, k, j:j + 1], in1=xc[:, k, 0:4], op0=ALU.mult, op1=ALU.add),
                           r=["scS0", "cw", "xc"], w=["xc"])
                for b in range(4):
                    for j in range(2):
                        dma("sp", lambda h, b=b, j=j: h.dma_start(out=cs[b, j].rearrange("(k p) -> p k", p=128), in_=scS0[:, :, (j + 1) * 4 + b]), r=["scS0"])
                    dma("sp", lambda h, b=b: h.dma_start(out=cs[b, 2].rearrange("(k p) -> p k", p=128), in_=XR[:, :, 3 + b]), r=["XR"])
                for k in range(4):
                    op("pe", lambda h, k=k: h.matmul(pA[:, k * 128:k * 128 + 4], lhsT=WA[:, k, :], rhs=xc[:, k, 0:4], start=True, stop=True), r=["WA", "xc"], w=["pA"])
                    op("pe", lambda h, k=k: h.matmul(pB[:, k * 128:k * 128 + 4], lhsT=WX[:, k, :], rhs=xc[:, k, 0:4], start=True, stop=True), r=["WX", "xc"], w=["pB"])
                for k in range(4):
                    op("act", lambda h, k=k: h.activation(out=rr[:, k, 0:4], in_=pA[:, k * 128:k * 128 + 4], func=AF.Sigmoid, bias=ba[:, k:k + 1]), r=["pA", "ba"], w=["rr"])
                    op("act", lambda h, k=k: h.activation(out=ii[:, k, 0:4], in_=pB[:, k * 128:k * 128 + 4], func=AF.Sigmoid, bias=bx[:, k:k + 1]), r=["pB", "bx"], w=["ii"])
                for k in range(4):
                    op("act", lambda h, k=k: h.activation(out=rr[:, k, 0:4], in_=rr[:, k, 0:4], func=AF.Exp, scale=c8[:, k:k + 1]), r=["rr", "c8"], w=["rr"])
                op("pool", lambda h: h.tensor_tensor(out=uu[:, :, 0:4], in0=rr[:, :, 0:4], in1=rr[:, :, 0:4], op=ALU.mult), r=["rr"], w=["uu"])
                op("act", lambda h: h.activation(out=uu[:, :, 0:4], in_=uu[:, :, 0:4], func=AF.Sqrt, scale=-1.0, bias=1.0), r=["uu"], w=["uu"])
                op("pool", lambda h: h.tensor_tensor(out=uu[:, :, 0:4], in0=uu[:, :, 0:4], in1=ii[:, :, 0:4], op=ALU.mult), r=["uu", "ii"], w=["uu"])
                op("pool", lambda h: h.tensor_tensor(out=uu[:, :, 0:4], in0=uu[:, :, 0:4], in1=xc[:, :, 0:4], op=ALU.mult), r=["uu", "xc"], w=["uu"])
                op("pool", lambda h: h.tensor_tensor(out=HH[:, :, 0:4], in0=rr[:, :, 0:4], in1=h0[:], op=ALU.mult), r=["rr", "h0"], w=["HH"])
                op("pool", lambda h: h.tensor_tensor(out=HH[:, :, 0:4], in0=HH[:, :, 0:4], in1=uu[:, :, 0:4], op=ALU.add), r=["HH", "uu"], w=["HH"])
                for b in range(4):
                    dma("sp", lambda h, b=b: h.dma_start(out=hs[b].rearrange("(k p) -> p k", p=128), in_=HH[:, :, b]), r=["HH"])
                t4 = ii[:, :, 0:4]; g4 = GR[:, :, 0:4]
                op("pool", lambda h: h.tensor_tensor(out=t4, in0=g4, in1=g4, op=ALU.mult), r=["GR"], w=["ii"])
                op("pool", lambda h: h.tensor_scalar(out=t4, in0=t4, scalar1=0.044715, scalar2=1.0, op0=ALU.mult, op1=ALU.add), r=["ii"], w=["ii"])
                op("pool", lambda h: h.tensor_tensor(out=t4, in0=t4, in1=g4, op=ALU.mult), r=["ii", "GR"], w=["ii"])
                op("act", lambda h: h.activation(out=t4, in_=t4, func=AF.Sigmoid, scale=1.5957691216057308), r=["ii"], w=["ii"])
                op("pool", lambda h: h.tensor_tensor(out=t4, in0=t4, in1=g4, op=ALU.mult), r=["ii", "GR"], w=["ii"])
                op("pool", lambda h: h.tensor_tensor(out=rnnT[:, :, 0:4], in0=t4, in1=HH[:, :, 0:4], op=ALU.mult), r=["ii", "HH"], w=["rnnT"])
                for k in range(4):
                    op("pe", lambda h, k=k: h.transpose(pT[0:4, k * 128:(k + 1) * 128], rnnT[:, k, 0:4], identb[:]), r=["rnnT", "identb"], w=["pT"])
                op("act", lambda h: h.activation(out=catR[0:4, :], in_=pT[0:4, 0:512], func=AF.Identity), r=["pT"], w=["catR"])
                dma("sp", lambda h: h.dma_start(out=catd[NT * 128:NT * 128 + 4, 512:1024], in_=catR[0:4, :]), r=["catR"], w=[("catd", NT, 1)])
            sc_.flush()

        if with_sample:
          with ExitStack() as es:
            def sb(n, s, d=F32):
                return es.enter_context(nc.sbuf_tensor(n, s, d))

            def ps(n, s, d=F32):
                return es.enter_context(nc.psum_tensor(n, s, d))
            NP_ = NPG
            kig = sb("kig", [128, 8192])
            KITo = [sb("KITo%d" % i, [128, 4, 128], BF16) for i in range(2)]
            identf = sb("identf", [128, 128]); onesf = sb("onesf", [128, 128]); sut = sb("sut", [128, 128])
            iota256 = sb("iota256", [128, 256]); iota32 = sb("iota32", [128, 32]); bthr = sb("bthr", [128, 31])
            selfm = sb("selfm", [128, 1]); posc = sb("posc", [128, 129]); offc = sb("offc", [128, 129])
            ones129 = sb("ones129", [128, 129])
            selr = sb("selr", [4, 4, 128]); selc = sb("selc", [128, 4, 4])
            sq = sb("sq", [4, 2120]); qiTs = sb("qiTs", [64, 8, 4]); qiTb = sb("qiTb", [64, 8, 4], BF16)
            ptcol = sb("ptcol", [128, 4], I32); pt128 = sb("pt128", [128, 4])
            wbc = sb("wbc", [128, 4, 8]); Rb = sb("Rb", [128, 512])
            scS = sb("scS", [128, 4, 130])
            pq = sb("pq", [4, 8, 64]); dself = sb("dself", [4, 8]); sself = sb("sself", [4, 1])
            los = sb("los", [128, 4]); mids = sb("mids", [128, 4]); cps = sb("cps", [128, 4]); ggs = sb("ggs", [128, 4]); junks = sb("junks", [128, 130], U8)
            mS = sb("mS", [128, 129]); cum = sb("cum", [128, 129]); slot = sb("slot", [128, 129])
            vals = sb("vals", [128, 129, 3]); OHt = [sb("OHt%d" % i, [128, 256]) for i in range(2)]
            gsel = sb("gsel", [128, 2, 3]); idx32 = sb("idx32", [128, 2], I32)
            Ksel = sb("Ksel", [128, 2, 512]); Vsel = sb("Vsel", [128, 2, 512])
            qbc = sb("qbc", [128, 512]); kbc = sb("kbc", [128, 512]); vbc = sb("vbc", [128, 512]); prod = sb("prod", [128, 512])
            lg = sb("lg", [128, 2, 8]); lgself = sb("lgself", [128, 8]); dlt = sb("dlt", [128, 8])
            isself = sb("isself", [128, 2]); dist = sb("dist", [128, 2])
            ge = sb("ge", [128, 31]); bk = sb("bk", [128, 1]); OHB = sb("OHB", [128, 32])
            rbT1 = sb("rbT1", [1, 256]); rbbc = sb("rbbc", [128, 8, 32]); pbias = sb("pbias", [128, 8, 32]); bias_t = sb("bias_t", [128, 2, 8])
            pp = sb("pp", [128, 2, 8]); pv = sb("pv", [128, 2, 512])
            rd = sb("rd", [4, 8]); cats = sb("cats", [4, 512], BF16)

            pTf = ps("pTf", [128, 512]); psc = [ps("psc%d" % i, [128, 512]) for i in range(2)]
            pX = ps("pX", [128, 512]); pG = [ps("pG%d" % i, [128, 512]) for i in range(2)]
            pN = ps("pN", [128, 512]); pD = ps("pD", [128, 512])

            for nm, tl in (("ident", identf), ("ones", onesf), ("sut", sut), ("iota256", iota256), ("iota32", iota32), ("bthr", bthr), ("selfm", selfm), ("posc", posc), ("offc", offc)):
                dma("sp", lambda h, nm=nm, tl=tl: h.dma_start(out=tl[:], in_=C[nm]), w=[nm + "_s"])
            op("dve", lambda h: h.memset(ones129[:], 1.0), w=["ones129"])
            op("dve", lambda h: h.tensor_copy(out=selr[:], in_=identf[0:4, 0:4].unsqueeze(2).to_broadcast([4, 4, 128])), r=["ident_s"], w=["selr"])
            op("dve", lambda h: h.memset(selc[:], 0.0), w=["selc"])
            for b in range(4):
                op("dve", lambda h, b=b: h.memset(selc[:, b, b:b + 1], 1.0), w=["selc"])
            dma("sp", lambda h: h.dma_start(out=sq[:], in_=sst), r=["sst"], w=["sq"])
            for b in range(4):
                dma("sp", lambda h, b=b: h.dma_start(out=qiTs[:, :, b], in_=sst[b, 1536:2048].rearrange("(a p) -> p a", p=64)), r=["sst"], w=["qiTs"])
                dma("sp", lambda h, b=b: h.dma_start(out=ptcol[0:NP_, b:b + 1], in_=pt[b, :].rearrange("(p o) -> p o", o=1)), w=["ptcol"])
            op("dve", lambda h: h.tensor_copy(out=qiTb[:], in_=qiTs[:]), r=["qiTs"], w=["qiTb"])
            op("dve", lambda h: h.memset(pt128[:], 0.0), w=["pt128"])
            op("dve", lambda h: h.tensor_copy(out=pt128[0:NP_, :], in_=ptcol[0:NP_, :]), r=["ptcol"], w=["pt128"])
            op("dve", lambda h: h.tensor_scalar(out=pt128[0:NP_, :], in0=pt128[0:NP_, :], scalar1=128.0, scalar2=None, op0=ALU.mult), r=["pt128"], w=["pt128"])
            dma("sp", lambda h: h.dma_start(out=rbT1[0:1, :].rearrange("o (a b) -> o a b", a=8), in_=relb.rearrange("(o b) a -> o a b", o=1)), w=["rbT1"])
            op("pe", lambda h: h.matmul(pX[:, 0:256], lhsT=onesf[0:1, :], rhs=rbT1[0:1, :], start=True, stop=True), r=["ones_s", "rbT1"], w=["pX"])
            op("act", lambda h: h.activation(out=rbbc[:].rearrange("p a b -> p (a b)"), in_=pX[:, 0:256], func=AF.Identity), r=["pX"], w=["rbbc"])
            op("dve", lambda h: h.memset(scS[:], NEG), w=["scS"])

            def bcast(dst_ps, b, rhs_ap, n, rk):
                op("pe", lambda h: h.matmul(dst_ps[:, 0:n], lhsT=selr[0:4, b, :], rhs=rhs_ap, start=True, stop=True), r=["selr"] + rk, w=["pX"])

            for b in range(4):
                bcast(pX, b, sq[0:4, 2112:2120], 8, ["sq"])
                op("dve", lambda h, b=b: h.tensor_copy(out=wbc[:, b, :], in_=pX[:, 0:8]), r=["pX"], w=["wbc"])
            op("dve", lambda h: h.tensor_tensor(out=pq[:], in0=sq[0:4, 1536:2048].rearrange("p (a d) -> p a d", a=8), in1=sq[0:4, 2048:2112].unsqueeze(1).to_broadcast([4, 8, 64]), op=ALU.mult), r=["sq"], w=["pq"])
            op("dve", lambda h: h.tensor_reduce(out=dself[:], in_=pq[:], axis=AX.X, op=ALU.add), r=["pq"], w=["dself"])
            op("dve", lambda h: h.tensor_scalar(out=dself[:], in0=dself[:], scalar1=0.0, scalar2=None, op0=ALU.max), r=["dself"], w=["dself"])
            op("dve", lambda h: h.tensor_tensor(out=dself[:], in0=dself[:], in1=sq[0:4, 2112:2120], op=ALU.mult), r=["dself", "sq"], w=["dself"])
            op("dve", lambda h: h.tensor_reduce(out=sself[:], in_=dself[:], axis=AX.X, op=ALU.add), r=["dself"], w=["sself"])
            for b in range(4):
                bcast(pX, b, sself[0:4, 0:1], 1, ["sself"])
                op("dve", lambda h, b=b: h.tensor_scalar(out=scS[:, b, 128:129], in0=pX[:, 0:1], scalar1=selfm[:, 0:1], scalar2=None, op0=ALU.add), r=["pX", "selfm_s"], w=["scS"])
            ko = 0
            for b in range(4):
                dma("pool", lambda h, b=b: h.indirect_dma_start(out=kig[0:NP_, :], out_offset=None, in_=cki.rearrange("(n p) d -> n (p d)", p=128),
                                                              in_offset=bass.IndirectOffsetOnAxis(ap=ptcol[0:NP_, b:b + 1], axis=0)), r=["ptcol"], w=["kig"])
                for half in range(2):
                    for og in range(16):
                        kt = KITo[ko % 2]; ktk = "KITo%d" % (ko % 2); ko += 1
                        for o4 in range(4):
                            o = half * 64 + og * 4 + o4
                            op("pe", lambda h, o=o, o4=o4: h.transpose(pTf[0:64, o4 * 128:o4 * 128 + NP_], kig[0:NP_, o * 64:(o + 1) * 64], identf[0:NP_, 0:NP_]), r=["kig", "ident_s"], w=["pTf"])
                        ev = "act" if og % 2 else "dve"
                        if ev == "act":
                            op("act", lambda h, kt=kt: h.activation(out=kt[0:64, :, 0:NP_], in_=pTf[0:64, :].rearrange("p (a t) -> p a t", a=4)[:, :, 0:NP_], func=AF.Identity), r=["pTf"], w=[ktk])
                        else:
                            op("dve", lambda h, kt=kt: h.tensor_copy(out=kt[0:64, :, 0:NP_], in_=pTf[0:64, :].rearrange("p (a t) -> p a t", a=4)[:, :, 0:NP_]), r=["pTf"], w=[ktk])
                        for o4 in range(4):
                            ol = og * 4 + o4
                            op("pe", lambda h, kt=kt, o4=o4, ol=ol, half=half, b=b: h.matmul(psc[half][0:NP_, ol * 8:(ol + 1) * 8], lhsT=kt[0:64, o4, 0:NP_], rhs=qiTb[0:64, :, b], start=True, stop=True),
                               r=[ktk, "qiTb"], w=["psc%d" % half])
                    op("act", lambda h, half=half: h.activation(out=Rb[0:NP_, :], in_=psc[half][0:NP_, :], func=AF.Relu), r=["psc%d" % half], w=["Rb"])
                    op("dve", lambda h, b=b: h.tensor_tensor(out=Rb[0:NP_, :].rearrange("p (o a) -> p o a", a=8), in0=Rb[0:NP_, :].rearrange("p (o a) -> p o a", a=8),
                                                            in1=wbc[0:NP_, b, :].unsqueeze(1).to_broadcast([NP_, 64, 8]), op=ALU.mult), r=["Rb", "wbc"], w=["Rb"])
                    op("dve", lambda h, b=b, half=half: h.tensor_reduce(out=scS[0:NP_, b, half * 64:(half + 1) * 64], in_=Rb[0:NP_, :].rearrange("p (o a) -> p o a", a=8), axis=AX.X, op=ALU.add),
                       r=["Rb"], w=["scS"])
            op("dve", lambda h: h.memset(los[:], -1024.0), w=["los"])
            for it in range(NIT):
                hk = 1024.0 / (2.0 ** it)
                op("dve", lambda h, hk=hk: h.tensor_scalar(out=mids[:], in0=los[:], scalar1=hk, scalar2=None, op0=ALU.add), r=["los"], w=["mids"])
                for b in range(4):
                    op("dve", lambda h, b=b: h.tensor_scalar(out=junks[:, 0:129], in0=scS[:, b, 0:129], scalar1=mids[:, b:b + 1], scalar2=None, op0=ALU.is_gt, op1=ALU.add, accum_out=cps[:, b:b + 1]),
                       r=["scS", "mids"], w=["junks", "cps"])
                op("pe", lambda h: h.matmul(pX[:, 0:4], lhsT=onesf[:, :], rhs=cps[:, 0:4], start=True, stop=True), r=["ones_s", "cps"], w=["pX"])
                op("dve", lambda h, hk=hk: h.tensor_scalar(out=ggs[:], in0=pX[:, 0:4], scalar1=float(TOPK_S), scalar2=hk, op0=ALU.is_ge, op1=ALU.mult), r=["pX"], w=["ggs"])
                op("dve", lambda h: h.tensor_tensor(out=los[:], in0=los[:], in1=ggs[:], op=ALU.add), r=["los", "ggs"], w=["los"])
            op("dve", lambda h: h.tensor_copy(out=vals[:, :, 1], in_=posc[:]), r=["posc_s"], w=["vals"])
            op("dve", lambda h: h.memset(vals[:, :, 2], 1.0), w=["vals"])
            oi = 0
            for b in range(4):
                op("dve", lambda h, b=b: h.tensor_scalar(out=mS[:], in0=scS[:, b, 0:129], scalar1=los[:, b:b + 1], scalar2=None, op0=ALU.is_gt), r=["scS", "los"], w=["mS"])
                op("dve", lambda h: h.tensor_tensor_scan(out=cum[:], data0=ones129[:], data1=mS[:], initial=0.0, op0=ALU.mult, op1=ALU.add), r=["ones129", "mS"], w=["cum"])
                op("pe", lambda h: h.matmul(pX[:, 0:1], lhsT=sut[:, :], rhs=cum[:, 128:129], start=True, stop=True), r=["sut_s", "cum"], w=["pX"])
                op("dve", lambda h: h.tensor_scalar(out=slot[:], in0=cum[:], scalar1=pX[:, 0:1], scalar2=None, op0=ALU.add), r=["cum", "pX"], w=["slot"])
                op("dve", lambda h: h.tensor_tensor(out=slot[:], in0=slot[:], in1=mS[:], op=ALU.mult), r=["slot", "mS"], w=["slot"])
                op("dve", lambda h: h.tensor_scalar(out=slot[:], in0=slot[:], scalar1=-1.0, scalar2=None, op0=ALU.add), r=["slot"], w=["slot"])
                op("dve", lambda h, b=b: h.tensor_scalar(out=vals[:, :, 0], in0=offc[:], scalar1=pt128[:, b:b + 1], scalar2=None, op0=ALU.add), r=["offc_s", "pt128"], w=["vals"])
                op("dve", lambda h: h.memset(vals[:, 128, 0:1], 0.0), w=["vals"])
                for o in range(129):
                    oh = OHt[oi % 2]; ohk = "OHt%d" % (oi % 2); oi += 1
                    op("dve", lambda h, oh=oh, o=o: h.tensor_scalar(out=oh[:], in0=iota256[:], scalar1=slot[:, o:o + 1], scalar2=None, op0=ALU.is_equal), r=["iota256_s", "slot"], w=[ohk])
                    for half in range(2):
                        op("pe", lambda h, oh=oh, o=o, half=half: h.matmul(pG[half][:, 0:3], lhsT=oh[:, half * 128:(half + 1) * 128], rhs=vals[:, o, :], start=(o == 0), stop=(o == 128)),
                           r=[ohk, "vals"], w=["pG%d" % half])
                for half in range(2):
                    op("dve", lambda h, half=half: h.tensor_copy(out=gsel[:, half, :], in_=pG[half][:, 0:3]), r=["pG%d" % half], w=["gsel"])
                op("dve", lambda h: h.tensor_copy(out=idx32[:], in_=gsel[:, :, 0]), r=["gsel"], w=["idx32"])
                for half in range(2):
                    dma("pool", lambda h, half=half: h.indirect_dma_start(out=Ksel[:, half, :], out_offset=None, in_=ck, in_offset=bass.IndirectOffsetOnAxis(ap=idx32[:, half:half + 1], axis=0)),
                        r=["idx32"], w=["Ksel"])
                    dma("pool", lambda h, half=half: h.indirect_dma_start(out=Vsel[:, half, :], out_offset=None, in_=cv, in_offset=bass.IndirectOffsetOnAxis(ap=idx32[:, half:half + 1], axis=0)),
                        r=["idx32"], w=["Vsel"])
                for (c0, dst, dk) in ((0, qbc, "qbc"), (512, kbc, "kbc"), (1024, vbc, "vbc")):
                    bcast(pX, b, sq[0:4, c0:c0 + 512], 512, ["sq"])
                    op("act", lambda h, dst=dst: h.activation(out=dst[:], in_=pX[:, :], func=AF.Identity), r=["pX"], w=[dk])
                op("dve", lambda h: h.tensor_scalar(out=isself[:], in0=gsel[:, :, 1], scalar1=float(PAST), scalar2=None, op0=ALU.is_equal), r=["gsel"], w=["isself"])
                op("dve", lambda h: h.tensor_scalar(out=dist[:], in0=gsel[:, :, 1], scalar1=-1.0, scalar2=float(PAST), op0=ALU.mult, op1=ALU.add), r=["gsel"], w=["dist"])
                op("dve", lambda h: h.tensor_tensor(out=prod[:], in0=kbc[:], in1=qbc[:], op=ALU.mult), r=["kbc", "qbc"], w=["prod"])
                op("dve", lambda h: h.tensor_reduce(out=lgself[:], in_=prod[:].rearrange("p (a d) -> p a d", a=8), axis=AX.X, op=ALU.add), r=["prod"], w=["lgself"])
                for half in range(2):
                    op("dve", lambda h, half=half: h.tensor_tensor(out=prod[:], in0=Ksel[:, half, :], in1=qbc[:], op=ALU.mult), r=["Ksel", "qbc"], w=["prod"])
                    op("dve", lambda h, half=half: h.tensor_reduce(out=lg[:, half, :], in_=prod[:].rearrange("p (a d) -> p a d", a=8), axis=AX.X, op=ALU.add), r=["prod"], w=["lg"])
                    op("dve", lambda h, half=half: h.tensor_tensor(out=dlt[:], in0=lgself[:], in1=lg[:, half, :], op=ALU.subtract), r=["lgself", "lg"], w=["dlt"])
                    op("dve", lambda h, half=half: h.scalar_tensor_tensor(out=lg[:, half, :], in0=dlt[:], scalar=isself[:, half:half + 1], in1=lg[:, half, :], op0=ALU.mult, op1=ALU.add),
                       r=["dlt", "isself", "lg"], w=["lg"])
                    op("dve", lambda h, half=half: h.tensor_scalar(out=ge[:], in0=bthr[:], scalar1=dist[:, half:half + 1], scalar2=None, op0=ALU.is_le), r=["bthr_s", "dist"], w=["ge"])
                    op("dve", lambda h: h.tensor_reduce(out=bk[:], in_=ge[:], axis=AX.X, op=ALU.add), r=["ge"], w=["bk"])
                    op("dve", lambda h: h.tensor_scalar(out=OHB[:], in0=iota32[:], scalar1=bk[:, 0:1], scalar2=None, op0=ALU.is_equal), r=["iota32_s", "bk"], w=["OHB"])
                    op("dve", lambda h: h.tensor_tensor(out=pbias[:], in0=rbbc[:], in1=OHB[:].unsqueeze(1).to_broadcast([128, 8, 32]), op=ALU.mult), r=["rbbc", "OHB"], w=["pbias"])
                    op("dve", lambda h, half=half: h.tensor_reduce(out=bias_t[:, half, :], in_=pbias[:], axis=AX.X, op=ALU.add), r=["pbias"], w=["bias_t"])
                    op("pool", lambda h, half=half: h.tensor_tensor(out=prod[:], in0=vbc[:], in1=Vsel[:, half, :], op=ALU.subtract), r=["vbc", "Vsel", "lg"], w=["prod"])
                    op("dve", lambda h, half=half: h.scalar_tensor_tensor(out=Vsel[:, half, :], in0=prod[:], scalar=isself[:, half:half + 1], in1=Vsel[:, half, :], op0=ALU.mult, op1=ALU.add),
                       r=["prod", "isself", "Vsel"], w=["Vsel"])
                op("dve", lambda h: h.scalar_tensor_tensor(out=lg[:], in0=lg[:], scalar=0.125, in1=bias_t[:], op0=ALU.mult, op1=ALU.add), r=["lg", "bias_t"], w=["lg"])
                op("act", lambda h: h.activation(out=pp[:], in_=lg[:], func=AF.Exp), r=["lg"], w=["pp"])
                op("dve", lambda h: h.tensor_tensor(out=pp[:], in0=pp[:], in1=gsel[:, :, 2:3].to_broadcast([128, 2, 8]), op=ALU.mult), r=["pp", "gsel"], w=["pp"])
                for half in range(2):
                    op("dve", lambda h, half=half: h.tensor_tensor(out=pv[:, half, :].rearrange("p (a d) -> p a d", a=8), in0=Vsel[:, half, :].rearrange("p (a d) -> p a d", a=8),
                                                                in1=pp[:, half, :].unsqueeze(2).to_broadcast([128, 8, 64]), op=ALU.mult), r=["Vsel", "pp"], w=["pv"])
                for half in range(2):
                    st_ = (b == 0 and half == 0); sp_ = (b == 3 and half == 1)
                    op("pe", lambda h, b=b, half=half, st_=st_, sp_=sp_: h.matmul(pN[0:4, 0:512], lhsT=selc[:, b, :], rhs=pv[:, half, :], start=st_, stop=sp_), r=["selc", "pv"], w=["pN"])
                    op("pe", lambda h, b=b, half=half, st_=st_, sp_=sp_: h.matmul(pD[0:4, 0:8], lhsT=selc[:, b, :], rhs=pp[:, half, :], start=st_, stop=sp_), r=["selc", "pp"], w=["pD"])
            op("dve", lambda h: h.reciprocal(out=rd[:], in_=pD[0:4, 0:8]), r=["pD"], w=["rd"])
            op("dve", lambda h: h.tensor_tensor(out=cats[:].rearrange("p (a d) -> p a d", a=8), in0=pN[0:4, 0:512].rearrange("p (a d) -> p a d", a=8), in1=rd[:].unsqueeze(2).to_broadcast([4, 8, 64]), op=ALU.mult),
               r=["pN", "rd"], w=["cats"])
            dma("sp", lambda h: h.dma_start(out=catd[NT * 128:NT * 128 + 4, 0:512], in_=cats[:]), r=["cats"], w=[("catd", NT, 0)])
            sc_.flush()

        if with_passB:
          with ExitStack() as es:
            def sb(n, s, d=F32):
                return es.enter_context(nc.sbuf_tensor(n, s, d))

            def ps(n, s, d=F32):
                return es.enter_context(nc.psum_tensor(n, s, d))
            wout = sb("wout", [128, 8, D], BF16)
            wup = sb("wup", [128, 8, DFF], BF16)
            wdn = sb("wdn", [128, 32, D], BF16)
            G1 = sb("G1", [128, D]); B1 = sb("B1", [128, D]); G2 = sb("G2", [128, D]); B2 = sb("B2", [128, D]); BD = sb("BD", [128, D])
            bup = sb("bup", [128, 32])
            catb = sb("catb", [128, D], BF16)
            catT = sb("catT", [128, 8, 128], BF16)
            xin2 = sb("xin2", [128, D])
            x1 = sb("x1", [128, D]); x1b = sb("x1b", [128, D], BF16); x1T = sb("x1T", [128, 8, 128], BF16)
            hidT = sb("hidT", [128, 32, 128], BF16)
            rl = [sb("rl%d" % i, [128, 512]) for i in range(2)]
            yt = sb("yt", [128, D])
            st = sb("st", [128, 2, 6]); mv = sb("mv", [128, 2]); sd = sb("sd", [128, 1])
            pM = [ps("pM%d" % i, [128, 512]) for i in range(2)]
            pU = [ps("pU%d" % i, [128, 512]) for i in range(2)]
            pT2 = ps("pT2", [128, 1024], BF16)

            wo_v = w_out.rearrange("(c p) n -> p c n", p=128)
            wu_v = w_up.rearrange("(c p) n -> p c n", p=128)
            wd_v = w_down.rearrange("(c p) n -> p c n", p=128)
            for c in range(8):
                dma("pool", lambda h, c=c: h.dma_start(out=wout[:, c, :], in_=wo_v[:, c, :]), w=["wout"])
            for c in range(8):
                for hf in range(2):
                    dma("pool", lambda h, c=c, hf=hf: h.dma_start(out=wup[:, c, hf * 2048:(hf + 1) * 2048], in_=wu_v[:, c, hf * 2048:(hf + 1) * 2048]), w=["wup"])
            for c in range(32):
                dma("pool", lambda h, c=c: h.dma_start(out=wdn[:, c, :], in_=wd_v[:, c, :]), w=["wdn"])
            for nm, tl, src in (("G1", G1, ln1_g), ("B1", B1, ln1_b), ("G2", G2, ln2_g), ("B2", B2, ln2_b), ("BD", BD, b_down)):
                dma("sp", lambda h, tl=tl, src=src: h.dma_start(out=tl[:], in_=src.unsqueeze(0).to_broadcast([128, D])), w=[nm])
            dma("sp", lambda h: h.dma_start(out=bup[:], in_=b_up.rearrange("(f p) -> p f", p=128)), w=["bup"])

            def layer_norm(src, dst, Gt, Bt, gk, bk, R, srck, dstk):
                for hf in range(2):
                    op("dve", lambda h, hf=hf: h.bn_stats(out=st[0:R, hf, :], in_=src[0:R, hf * 512:(hf + 1) * 512]), r=[srck], w=["st"])
                op("dve", lambda h: h.bn_aggr(out=mv[0:R, :], in_=st[0:R, :, :].rearrange("p a b -> p (a b)")), r=["st"], w=["mv"])
                op("act", lambda h: h.activation(out=sd[0:R, :], in_=mv[0:R, 1:2], func=AF.Sqrt, bias=EPS), r=["mv"], w=["sd"])
                op("dve", lambda h: h.reciprocal(out=sd[0:R, :], in_=sd[0:R, :]), r=["sd"], w=["sd"])
                op("dve", lambda h: h.tensor_scalar(out=dst[0:R, :], in0=src[0:R, :], scalar1=mv[0:R, 0:1], scalar2=sd[0:R, 0:1], op0=ALU.subtract, op1=ALU.mult),
                   r=[srck, "mv", "sd"], w=[dstk])
                op("dve", lambda h: h.tensor_tensor(out=dst[0:R, :], in0=dst[0:R, :], in1=Gt[0:R, :], op=ALU.mult), r=[dstk, gk], w=[dstk])
                op("pool", lambda h: h.tensor_tensor(out=dst[0:R, :], in0=dst[0:R, :], in1=Bt[0:R, :], op=ALU.add), r=[dstk, bk], w=[dstk])

            tiles = list(range(NT)) + ([NT] if with_sample else [])
            for i in tiles:
                R = 128 if i < NT else 4
                xsrc = x[i * 128:(i + 1) * 128, :] if i < NT else xs
                ydst = y[i * 128:(i + 1) * 128, :] if i < NT else ys
                dma("sp", lambda h, i=i, R=R: h.dma_start(out=catb[0:R, :], in_=catd[i * 128:i * 128 + R, :]), r=[("catd", i, 0), ("catd", i, 1)], w=["catb"])
                dma("sp", lambda h, R=R, xsrc=xsrc: h.dma_start(out=xin2[0:R, :], in_=xsrc), w=["xin2"])
                for c in range(8):
                    op("pe", lambda h, c=c, R=R: h.transpose(pT2[:, c * 128:c * 128 + R], catb[0:R, c * 128:(c + 1) * 128], identb[0:R, 0:R]), r=["catb", "identb"], w=["pT2"])
                op("act", lambda h, R=R: h.activation(out=catT[:, :, 0:R], in_=pT2[:, :].rearrange("p (c t) -> p c t", c=8)[:, :, 0:R], func=AF.Identity), r=["pT2"], w=["catT"])
                for hf in range(2):
                    for c in range(8):
                        op("pe", lambda h, hf=hf, c=c, R=R: h.matmul(pM[hf][0:R, :], lhsT=catT[:, c, 0:R], rhs=wout[:, c, hf * 512:(hf + 1) * 512], start=(c == 0), stop=(c == 7)),
                           r=["catT", "wout"], w=["pM%d" % hf])
                for hf in range(2):
                    op("dve", lambda h, hf=hf, R=R: h.scalar_tensor_tensor(out=yt[0:R, hf * 512:(hf + 1) * 512], in0=xin2[0:R, hf * 512:(hf + 1) * 512], scalar=ALPHA, in1=pM[hf][0:R, :],
                                                                      op0=ALU.mult, op1=ALU.add), r=["xin2", "pM%d" % hf], w=["yt"])
                layer_norm(yt, x1, G1, B1, "G1", "B1", R, "yt", "x1")
                op("act", lambda h, R=R: h.activation(out=x1b[0:R, :], in_=x1[0:R, :], func=AF.Identity), r=["x1"], w=["x1b"])
                for c in range(8):
                    op("pe", lambda h, c=c, R=R: h.transpose(pT2[:, c * 128:c * 128 + R], x1b[0:R, c * 128:(c + 1) * 128], identb[0:R, 0:R]), r=["x1b", "identb"], w=["pT2"])
                op("dve", lambda h, R=R: h.tensor_copy(out=x1T[:, :, 0:R], in_=pT2[:, :].rearrange("p (c t) -> p c t", c=8)[:, :, 0:R]), r=["pT2"], w=["x1T"])
                for fg in range(8):
                    pu = pU[fg % 2]; puk = "pU%d" % (fg % 2); rlt = rl[fg % 2]; rlk = "rl%d" % (fg % 2)
                    for f4 in range(4):
                        f = fg * 4 + f4
                        for c in range(8):
                            op("pe", lambda h, pu=pu, f=f, f4=f4, c=c, R=R: h.matmul(pu[:, f4 * 128:f4 * 128 + R], lhsT=wup[:, c, f * 128:(f + 1) * 128], rhs=x1T[:, c, 0:R], start=(c == 0), stop=(c == 7)),
                               r=["wup", "x1T"], w=[puk])
                    for f4 in range(4):
                        f = fg * 4 + f4
                        op("act", lambda h, pu=pu, rlt=rlt, f=f, f4=f4, R=R: h.activation(out=rlt[:, f4 * 128:f4 * 128 + R], in_=pu[:, f4 * 128:f4 * 128 + R], func=AF.Relu, bias=bup[:, f:f + 1]),
                           r=[puk, "bup"], w=[rlk])
                    op("pool", lambda h, rlt=rlt, fg=fg, R=R: h.tensor_tensor(out=hidT[:, fg * 4:(fg + 1) * 4, 0:R], in0=rlt[:, :].rearrange("p (a t) -> p a t", a=4)[:, :, 0:R],
                                                                        in1=rlt[:, :].rearrange("p (a t) -> p a t", a=4)[:, :, 0:R], op=ALU.mult), r=[rlk], w=["hidT"])
                for hf in range(2):
                    for f in range(32):
                        op("pe", lambda h, hf=hf, f=f, R=R: h.matmul(pM[hf][0:R, :], lhsT=hidT[:, f, 0:R], rhs=wdn[:, f, hf * 512:(hf + 1) * 512], start=(f == 0), stop=(f == 31)),
                           r=["hidT", "wdn"], w=["pM%d" % hf])
                for hf in range(2):
                    op("dve", lambda h, hf=hf, R=R: h.scalar_tensor_tensor(out=yt[0:R, hf * 512:(hf + 1) * 512], in0=x1[0:R, hf * 512:(hf + 1) * 512], scalar=ALPHA, in1=pM[hf][0:R, :],
                                                                      op0=ALU.mult, op1=ALU.add), r=["x1", "pM%d" % hf], w=["yt"])
                op("pool", lambda h, R=R: h.tensor_tensor(out=yt[0:R, :], in0=yt[0:R, :], in1=BD[0:R, :], op=ALU.add), r=["yt", "BD"], w=["yt"])
                layer_norm(yt, x1, G2, B2, "G2", "B2", R, "yt", "x1")
                dma("sp", lambda h, R=R, ydst=ydst: h.dma_start(out=ydst, in_=x1[0:R, :]), r=["x1"])
            sc_.flush()
    return nc


def core_inputs(inp, c, consts):
    f = lambda a: np.ascontiguousarray(np.asarray(a))
    npool = inp["cache_k"].shape[1]
    m = {
        "x": f(inp["x_prompt"][c]),
        "xs": f(inp["x_sample"][4 * c:4 * c + 4, 0, :]),
        "ck": f(inp["cache_k"][0]).reshape(npool * 128, 512),
        "cv": f(inp["cache_v"][0]).reshape(npool * 128, 512),
        "cki": f(inp["cache_k_idx"][0]).reshape(npool * 128, 64),
        "sh": f(inp["state_h"][0, 4 * c:4 * c + 4]),
        "scv": f(inp["state_conv"][0, 4 * c:4 * c + 4]),
        "pt": f(inp["page_table"][4 * c:4 * c + 4]).astype(np.int32),
        "relb": f(inp["rel_bias"]),
        "w_in": f(inp["w_in"][0]), "conv_w": f(inp["conv_w"][0]), "conv_b": f(inp["conv_b"][0]),
        "w_a": f(inp["w_a"][0]), "b_a": f(inp["b_a"][0]).reshape(-1), "w_x": f(inp["w_x"][0]), "b_x": f(inp["b_x"][0]).reshape(-1),
        "lam": f(inp["lru_lambda"][0]), "w_out": f(inp["w_out"][0]), "ln1_g": f(inp["ln1_g"][0]), "ln1_b": f(inp["ln1_b"][0]),
        "w_up": f(inp["w_up"][0]), "b_up": f(inp["b_up"][0]), "w_down": f(inp["w_down"][0]), "b_down": f(inp["b_down"][0]),
        "ln2_g": f(inp["ln2_g"][0]), "ln2_b": f(inp["ln2_b"][0]),
    }
    for k, v in consts.items():
        m["c_" + k] = v
    return m


def assemble(results, B, S, NSB):
    cat = lambda k: np.stack([np.asarray(r[k]) for r in results], axis=0)
    y_p = cat("y")
    y_s = cat("ys").reshape(NSB, 1, D)
    k_p = cat("kp").reshape(1, B, S, 8, 64)
    v_p = cat("vp").reshape(1, B, S, 8, 64)
    ki_p = cat("kip").reshape(1, B, S, 64)
    h_p = cat("hp").reshape(1, B, 512)
    c_p = cat("cp").reshape(1, B, 3, 512)
    k_s = cat("ksm").reshape(1, NSB, 1, 8, 64)
    v_s = cat("vsm").reshape(1, NSB, 1, 8, 64)
    ki_s = cat("kis").reshape(1, NSB, 1, 64)
    h_s = cat("hs").reshape(1, NSB, 512)
    c_s = cat("cs").reshape(1, NSB, 3, 512)
    return tuple(np.ascontiguousarray(a, dtype=np.float32) for a in (y_p, y_s, k_p, v_p, ki_p, h_p, c_p, k_s, v_s, ki_s, h_s, c_s))


def kernel(**inputs):
    B, S, _ = inputs["x_prompt"].shape
    NSB = inputs["x_sample"].shape[0]
    NPG = inputs["page_table"].shape[1]
    NPOOL = inputs["cache_k"].shape[1]
    ncores = B
    assert NSB == 4 * ncores
    nc = build(S, NPG, NPOOL)
    consts = make_consts(NPG)
    in_maps = [core_inputs(inputs, c, consts) for c in range(ncores)]
    res = run_bass_kernel_spmd(nc, in_maps, core_ids=list(range(ncores)))
    return assemble(res.results, B, S, NSB)
```
